# Optimizing a Trainium2 kernel written in Bass

```python
import math
import jax
import jax.numpy as jnp
from jax import lax
import numpy as np

D_MODEL = 1024
BATCH = 16
SEQ = 256
DEPTH = 4
DEC_BATCH = 8
DEC_SEQ = 2048
PAST_LEN = 256

GRID_W = 64
N_MIXERS = 4
N_LAYERS_MLA = len(range(0, DEPTH, N_MIXERS))
N_LAYERS_DIFF = len(range(1, DEPTH, N_MIXERS))
N_LAYERS_SCONV = len(range(2, DEPTH, N_MIXERS))
N_LAYERS_GMLP = len(range(3, DEPTH, N_MIXERS))
EPS = 1e-6
ROPE_THETA = 10000.0
Q_BLOCK = 128
MLA_HEADS = 8
MLA_NOPE = 128
MLA_ROPE = 64
MLA_V = 128
MLA_Q_LORA = 768
MLA_KV_LORA = 256
DIFF_HEADS = 8
DIFF_HD = D_MODEL // (2 * DIFF_HEADS)
GMLP_WIDTH = D_MODEL
GMLP_CHUNK = 128
GMLP_GROUPS = 8
FFN_HIDDEN = 2816
N_MOD = 6

kernel_name = "hybrid_diffusion_prefix_step"


def _rms(x, g):
    xf = x.astype(jnp.float32)
    y = xf * lax.rsqrt(jnp.mean(xf * xf, axis=-1, keepdims=True) + EPS)
    return (y * g.astype(jnp.float32)).astype(x.dtype)


def _modulate(x, g, shift, scale):
    return _rms(x, g) * (1 + scale) + shift


def _axial_rope_tables(s, rope_dim):
    rows = s // GRID_W
    row = jnp.repeat(jnp.arange(rows, dtype=jnp.float32), GRID_W)
    col = jnp.tile(jnp.arange(GRID_W, dtype=jnp.float32), rows)
    n_freq = rope_dim // 4
    inv_freq = ROPE_THETA ** (-jnp.arange(n_freq, dtype=jnp.float32) / n_freq)
    ang = jnp.concatenate([row[:, None] * inv_freq, col[:, None] * inv_freq], axis=-1)
    return jnp.cos(ang), jnp.sin(ang)


def _rope(x, cos, sin):
    s, half = cos.shape
    bshape = (1, s) + (1,) * (x.ndim - 3) + (half,)
    cos = cos.reshape(bshape)
    sin = sin.reshape(bshape)
    xf = x.astype(jnp.float32)
    x1, x2 = xf[..., :half], xf[..., half:]
    return jnp.concatenate([x1 * cos - x2 * sin, x2 * cos + x1 * sin], axis=-1).astype(x.dtype)


def _dwconv3(x, w):
    xp = jnp.pad(x, ((0, 0), (1, 1), (0, 0)))
    return xp[:, :-2] * w[0] + xp[:, 1:-1] * w[1] + xp[:, 2:] * w[2]


def _attend(q, k, v):
    b, sq, g, h, d = q.shape
    nb = sq // Q_BLOCK
    scale = d ** -0.5
    qb = jnp.moveaxis(q.reshape(b, nb, Q_BLOCK, g, h, d), 1, 0)

    def block(qi):
        s = jnp.einsum('bqghd,bkghd->bghqk', qi, k).astype(jnp.float32) * scale
        p = jax.nn.softmax(s, axis=-1).astype(v.dtype)
        return jnp.einsum('bghqk,bkhe->bqghe', p, v)

    o = lax.map(block, qb)
    return jnp.moveaxis(o, 0, 1).reshape(b, sq, g, h, v.shape[-1])


def _mla(h, rope, ctx, w_down, q_norm, kv_norm, w_uq, w_uk, w_uv, qn_nope, qn_rope, kn_nope, kn_rope, w_o):
    b, s, _ = h.shape
    c_q, c_kv, k_rope = jnp.split(h @ w_down, [MLA_Q_LORA, MLA_Q_LORA + MLA_KV_LORA], axis=-1)
    q = (_rms(c_q, q_norm) @ w_uq).reshape(b, s, MLA_HEADS, MLA_NOPE + MLA_ROPE)
    q_nope = _rms(q[..., :MLA_NOPE], qn_nope)
    q_rope = _rms(q[..., MLA_NOPE:], qn_rope)
    c_kv = _rms(c_kv, kv_norm)
    k_rope = _rms(k_rope, kn_rope)
    if rope is None:
        ckv_all, kr_all = c_kv, k_rope
    else:
        q_rope = _rope(q_rope, *rope)
        ckv_all = jnp.concatenate([ctx[0], c_kv], axis=1)
        kr_all = jnp.concatenate([ctx[1], _rope(k_rope, *rope)], axis=1)
    l = ckv_all.shape[1]
    k_nope = _rms((ckv_all @ w_uk).reshape(b, l, MLA_HEADS, MLA_NOPE), kn_nope)
    k_pe = jnp.broadcast_to(kr_all[:, :, None, :], (b, l, MLA_HEADS, MLA_ROPE))
    k = jnp.concatenate([k_nope, k_pe], axis=-1)[:, :, None]
    v = (ckv_all @ w_uv).reshape(b, l, MLA_HEADS, MLA_V)
    qf = jnp.concatenate([q_nope, q_rope], axis=-1)[:, :, None]
    o = _attend(qf, k, v).reshape(b, s, MLA_HEADS * MLA_V)
    return o @ w_o, c_kv, k_rope


def _diff(h, rope, ctx, lam_init, w_qkv, qn, kn, lq1, lk1, lq2, lk2, head_norm, w_o):
    b, s, _ = h.shape
    q, k, v = jnp.split(h @ w_qkv, 3, axis=-1)
    q = _rms(q.reshape(b, s, 2, DIFF_HEADS, DIFF_HD), qn)
    k = _rms(k.reshape(b, s, 2, DIFF_HEADS, DIFF_HD), kn)
    v = v.reshape(b, s, DIFF_HEADS, 2 * DIFF_HD)
    if rope is None:
        k_all, v_all = k, v
    else:
        q = _rope(q, *rope)
        k_all = jnp.concatenate([ctx[0], _rope(k, *rope)], axis=1)
        v_all = jnp.concatenate([ctx[1], v], axis=1)
    o = _attend(q, k_all, v_all)
    lam = (jnp.exp(jnp.sum(lq1.astype(jnp.float32) * lk1.astype(jnp.float32)))
           - jnp.exp(jnp.sum(lq2.astype(jnp.float32) * lk2.astype(jnp.float32))) + lam_init).astype(o.dtype)
    o = o[:, :, 0] - lam * o[:, :, 1]
    o = _rms(o, head_norm) * (1.0 - lam_init)
    return o.reshape(b, s, D_MODEL) @ w_o, k, v


def _short_conv(h, w_in, conv_w, w_out):
    gb, gc, u = jnp.split(h @ w_in, 3, axis=-1)
    return (gb * _dwconv3(gc * u, conv_w)) @ w_out


def _chunk_gmlp(h, w_in, v_norm, w_s, b_s, w_out):
    b, s, _ = h.shape
    u, v = jnp.split(jax.nn.gelu(h @ w_in), 2, axis=-1)
    v = _rms(v, v_norm).reshape(b, s // GMLP_CHUNK, GMLP_CHUNK, GMLP_GROUPS, GMLP_WIDTH // GMLP_GROUPS)
    mixed = jnp.einsum('gpq,bcqge->bcpge', w_s, v) + b_s.T[None, None, :, :, None]
    return (u * mixed.reshape(b, s, GMLP_WIDTH)) @ w_out


def _conv_ffn(h, w_in, conv_w, conv_b, w_out):
    a = _dwconv3(h @ w_in, conv_w) + conv_b
    g, up = jnp.split(a, 2, axis=-1)
    return (jax.nn.silu(g) * up) @ w_out


def setup_inputs(seed: int = 0) -> dict:
    key = jax.random.key(seed)
    ks = iter(jax.random.split(key, 64))

    def nrm(shape, scale=1.0):
        return jax.random.normal(next(ks), shape, jnp.float32) * scale

    def gain(shape):
        return 1.0 + nrm(shape, 0.02)

    D = D_MODEL
    nA, nB, nC, nD = N_LAYERS_MLA, N_LAYERS_DIFF, N_LAYERS_SCONV, N_LAYERS_GMLP
    return {
        "x_prompt": nrm((BATCH, SEQ, D)),
        "x_sample": nrm((DEC_BATCH, DEC_SEQ, D)),
        "cache_mla_ckv": nrm((DEC_BATCH, nA, PAST_LEN, MLA_KV_LORA)),
        "cache_mla_krope": nrm((DEC_BATCH, nA, PAST_LEN, MLA_ROPE)),
        "cache_diff_k": nrm((DEC_BATCH, nB, PAST_LEN, 2, DIFF_HEADS, DIFF_HD)),
        "cache_diff_v": nrm((DEC_BATCH, nB, PAST_LEN, DIFF_HEADS, 2 * DIFF_HD)),
        "c": nrm((DEC_BATCH, D)),
        "c_ctx": nrm((D,)),
        "ada_w": nrm((DEPTH, D, N_MOD * D), 0.5 * D ** -0.5),
        "ada_b": nrm((DEPTH, N_MOD * D), 0.02),
        "norm1_g": gain((DEPTH, D)),
        "norm2_g": gain((DEPTH, D)),
        "mla_w_down": nrm((nA, D, MLA_Q_LORA + MLA_KV_LORA + MLA_ROPE), D ** -0.5),
        "mla_q_norm": gain((nA, MLA_Q_LORA)),
        "mla_kv_norm": gain((nA, MLA_KV_LORA)),
        "mla_w_uq": nrm((nA, MLA_Q_LORA, MLA_HEADS * (MLA_NOPE + MLA_ROPE)), MLA_Q_LORA ** -0.5),
        "mla_w_uk": nrm((nA, MLA_KV_LORA, MLA_HEADS * MLA_NOPE), MLA_KV_LORA ** -0.5),
        "mla_w_uv": nrm((nA, MLA_KV_LORA, MLA_HEADS * MLA_V), MLA_KV_LORA ** -0.5),
        "mla_qn_nope": gain((nA, MLA_NOPE)),
        "mla_qn_rope": gain((nA, MLA_ROPE)),
        "mla_kn_nope": gain((nA, MLA_NOPE)),
        "mla_kn_rope": gain((nA, MLA_ROPE)),
        "mla_w_o": nrm((nA, MLA_HEADS * MLA_V, D), (MLA_HEADS * MLA_V) ** -0.5),
        "diff_w_qkv": nrm((nB, D, 3 * D), D ** -0.5),
        "diff_qn": gain((nB, DIFF_HD)),
        "diff_kn": gain((nB, DIFF_HD)),
        "diff_lq1": nrm((nB, DIFF_HD), 0.1),
        "diff_lk1": nrm((nB, DIFF_HD), 0.1),
        "diff_lq2": nrm((nB, DIFF_HD), 0.1),
        "diff_lk2": nrm((nB, DIFF_HD), 0.1),
        "diff_head_norm": gain((nB, 2 * DIFF_HD)),
        "diff_w_o": nrm((nB, D, D), D ** -0.5),
        "sconv_w_in": nrm((nC, D, 3 * D), D ** -0.5),
        "sconv_w": nrm((nC, 3, D), 3 ** -0.5),
        "sconv_w_out": nrm((nC, D, D), D ** -0.5),
        "gmlp_w_in": nrm((nD, D, 2 * GMLP_WIDTH), D ** -0.5),
        "gmlp_v_norm": gain((nD, GMLP_WIDTH)),
        "gmlp_w_s": nrm((nD, GMLP_GROUPS, GMLP_CHUNK, GMLP_CHUNK), GMLP_CHUNK ** -0.5),
        "gmlp_b_s": 1.0 + nrm((nD, GMLP_GROUPS, GMLP_CHUNK), 0.01),
        "gmlp_w_out": nrm((nD, GMLP_WIDTH, D), GMLP_WIDTH ** -0.5),
        "ffn_w_in": nrm((DEPTH, D, 2 * FFN_HIDDEN), D ** -0.5),
        "ffn_conv_w": nrm((DEPTH, 3, 2 * FFN_HIDDEN), 3 ** -0.5),
        "ffn_conv_b": nrm((DEPTH, 2 * FFN_HIDDEN), 0.02),
        "ffn_w_out": nrm((DEPTH, FFN_HIDDEN, D), FFN_HIDDEN ** -0.5),
    }


def reference(x_prompt, x_sample, cache_mla_ckv, cache_mla_krope, cache_diff_k, cache_diff_v, c,
              c_ctx, ada_w, ada_b, norm1_g, norm2_g,
              mla_w_down, mla_q_norm, mla_kv_norm, mla_w_uq, mla_w_uk, mla_w_uv,
              mla_qn_nope, mla_qn_rope, mla_kn_nope, mla_kn_rope, mla_w_o,
              diff_w_qkv, diff_qn, diff_kn, diff_lq1, diff_lk1, diff_lq2, diff_lk2, diff_head_norm, diff_w_o,
              sconv_w_in, sconv_w, sconv_w_out,
              gmlp_w_in, gmlp_v_norm, gmlp_w_s, gmlp_b_s, gmlp_w_out,
              ffn_w_in, ffn_conv_w, ffn_conv_b, ffn_w_out):
    s_lat = x_sample.shape[1]
    rope_mla = _axial_rope_tables(s_lat, MLA_ROPE)
    rope_diff = _axial_rope_tables(s_lat, DIFF_HD)
    cond_ctx = jax.nn.silu(c_ctx)[None, :]
    cond_lat = jax.nn.silu(c)
    yp, ys = x_prompt, x_sample
    st_ckv, st_kr, st_dk, st_dv = [], [], [], []
    for i in range(DEPTH):
        kind, j = i % N_MIXERS, i // N_MIXERS
        mod_p = jnp.split((cond_ctx @ ada_w[i] + ada_b[i])[:, None, :], N_MOD, axis=-1)
        mod_s = jnp.split((cond_lat @ ada_w[i] + ada_b[i])[:, None, :], N_MOD, axis=-1)
        hp = _modulate(yp, norm1_g[i], mod_p[0], mod_p[1])
        hs = _modulate(ys, norm1_g[i], mod_s[0], mod_s[1])
        if kind == 0:
            prm = (mla_w_down[j], mla_q_norm[j], mla_kv_norm[j], mla_w_uq[j], mla_w_uk[j], mla_w_uv[j],
                   mla_qn_nope[j], mla_qn_rope[j], mla_kn_nope[j], mla_kn_rope[j], mla_w_o[j])
            mp, ckv_p, kr_p = _mla(hp, None, None, *prm)
            ms, _, _ = _mla(hs, rope_mla, (cache_mla_ckv[:, j], cache_mla_krope[:, j]), *prm)
            st_ckv.append(ckv_p)
            st_kr.append(kr_p)
        elif kind == 1:
            lam_init = 0.8 - 0.6 * math.exp(-0.3 * i)
            prm = (diff_w_qkv[j], diff_qn[j], diff_kn[j], diff_lq1[j], diff_lk1[j], diff_lq2[j], diff_lk2[j],
                   diff_head_norm[j], diff_w_o[j])
            mp, k_p, v_p = _diff(hp, None, None, lam_init, *prm)
            ms, _, _ = _diff(hs, rope_diff, (cache_diff_k[:, j], cache_diff_v[:, j]), lam_init, *prm)
            st_dk.append(k_p)
            st_dv.append(v_p)
        elif kind == 2:
            prm = (sconv_w_in[j], sconv_w[j], sconv_w_out[j])
            mp = _short_conv(hp, *prm)
            ms = _short_conv(hs, *prm)
        else:
            prm = (gmlp_w_in[j], gmlp_v_norm[j], gmlp_w_s[j], gmlp_b_s[j], gmlp_w_out[j])
            mp = _chunk_gmlp(hp, *prm)
            ms = _chunk_gmlp(hs, *prm)
        yp = yp + mod_p[2] * mp
        ys = ys + mod_s[2] * ms
        fprm = (ffn_w_in[i], ffn_conv_w[i], ffn_conv_b[i], ffn_w_out[i])
        yp = yp + mod_p[5] * _conv_ffn(_modulate(yp, norm2_g[i], mod_p[3], mod_p[4]), *fprm)
        ys = ys + mod_s[5] * _conv_ffn(_modulate(ys, norm2_g[i], mod_s[3], mod_s[4]), *fprm)
    state_mla_ckv = jnp.stack(st_ckv, axis=1)
    state_mla_krope = jnp.stack(st_kr, axis=1)
    state_diff_k = jnp.stack(st_dk, axis=1)
    state_diff_v = jnp.stack(st_dv, axis=1)
    return (yp, ys, state_mla_ckv, state_mla_krope, state_diff_k, state_diff_v)
```

```python
import math
import numpy as np
import concourse.bass as bass
import concourse.mybir as mybir
from concourse.bass_utils import run_bass_kernel_spmd

F32 = mybir.dt.float32
BF16 = mybir.dt.bfloat16
AF = mybir.ActivationFunctionType
ALU = mybir.AluOpType

D = 1024
DEPTH = 4
EPS = 1e-6
FH = 2816
NCORES = 8
SB_START = 16640
SB_END = 229344


class Buf:
    __slots__ = ("name", "w", "r", "_ds")

    def __init__(self, name):
        self.name = name
        self.w = None
        self.r = {}


class V:
    __slots__ = ("ap", "bufs")

    def __init__(self, ap, bufs):
        self.ap = ap
        self.bufs = bufs if isinstance(bufs, (list, tuple)) else [bufs]


class Sem:
    def __init__(self, nc, name, dma=False):
        self.h = nc.alloc_semaphore(name)
        self.n = 0
        self.dma = dma
        self.waited = False


class Eng:
    def __init__(self, nc, e, name, selfsync=True):
        self.e = e
        self.sem = Sem(nc, "p_" + name)
        self.seen = {}
        self.selfsync = selfsync
        self.name = name


class K:
    def __init__(self, nc):
        self.nc = nc
        self.PE = Eng(nc, nc.tensor, "pe", selfsync=False)
        self.ACT = Eng(nc, nc.scalar, "act")
        self.DVE = Eng(nc, nc.vector, "dve")
        self.POOL = Eng(nc, nc.gpsimd, "pool")
        self.SP = Eng(nc, nc.sync, "sp")
        self.engs = [self.PE, self.ACT, self.DVE, self.POOL, self.SP]
        self.sb_off = SB_START
        self.nalloc = 0
        self.dsems = {}

    def alloc(self, name, shape, dtype):
        nbytes = int(np.prod(shape[1:])) * (2 if dtype == BF16 else 4)
        off = (self.sb_off + 63) // 64 * 64
        assert off + nbytes <= SB_END, f"SBUF overflow allocating {name}: {off}+{nbytes}"
        self.sb_off = off + nbytes
        self.nalloc += 1
        return self.nc.alloc_sbuf_tensor_at(f"{name}_{self.nalloc}", list(shape), dtype, offset=off)

    def mark(self):
        return self.sb_off

    def release(self, m):
        self.sb_off = m

    def dsem(self, name):
        if name not in self.dsems:
            self.dsems[name] = Sem(self.nc, "d_" + name, dma=True)
        return self.dsems[name]

    def _waits(self, eng, reads, writes):
        need = {}

        def add(tok):
            if tok is None:
                return
            s, v = tok
            if need.get(s, 0) < v:
                need[s] = v

        for b in reads:
            add(b.w)
        for b in writes:
            add(b.w)
            for s, v in b.r.items():
                add((s, v))
        for s, v in need.items():
            if s is eng.sem and not eng.selfsync:
                continue
            if s.dma:
                v = s.n
                s.waited = True
            if eng.seen.get(s, 0) >= v:
                continue
            eng.e.wait_ge(s.h, v)
            eng.seen[s] = v

    def _record(self, tok, reads, writes):
        s, v = tok
        for b in reads:
            if b.r.get(s, 0) < v:
                b.r[s] = v
        for b in writes:
            b.w = tok
            b.r = {}

    def op(self, eng, outs, ins, fn):
        reads = [b for v in ins for b in v.bufs]
        writes = [b for v in outs for b in v.bufs]
        self._waits(eng, reads, writes)
        inst = fn()
        eng.sem.n += 1
        inst.then_inc(eng.sem.h, 1)
        self._record((eng.sem, eng.sem.n), reads, writes)

    def dma(self, eng, out, in_, sem, out_bufs=(), in_bufs=()):
        b0 = (list(out_bufs) + list(in_bufs))[0]
        if hasattr(b0, "_ds"):
            sem = b0._ds
        else:
            self._pool_i = getattr(self, "_pool_i", 0) + 1
            sem = self.dsem(f"pool_{eng.name}_{self._pool_i % 8}")
        if sem.waited and eng.seen.get(sem, 0) < sem.n:
            eng.e.wait_ge(sem.h, sem.n)
            eng.seen[sem] = sem.n
        sem.waited = False
        self._waits(eng, list(in_bufs), list(out_bufs))
        inst = eng.e.dma_start(out=out, in_=in_)
        sem.n += 16
        inst.then_inc(sem.h, 16)
        self._record((sem, sem.n), list(in_bufs), list(out_bufs))

    def barrier(self):
        for e in self.engs:
            for o in self.engs:
                if o is e or o.sem.n == 0:
                    continue
                if e.seen.get(o.sem, 0) < o.sem.n:
                    e.e.wait_ge(o.sem.h, o.sem.n)
                    e.seen[o.sem] = o.sem.n
            for s in self.dsems.values():
                if s.n and e.seen.get(s, 0) < s.n:
                    e.e.wait_ge(s.h, s.n)
                    e.seen[s] = s.n
                    s.waited = True

    def mm(self, ps, pairs, eng=None):
        nc = self.nc
        ins = [x for p in pairs for x in p]

        def fn():
            inst = None
            n = len(pairs)
            for i, (l, r) in enumerate(pairs):
                inst = nc.tensor.matmul(ps.ap, lhsT=l.ap, rhs=r.ap, start=(i == 0), stop=(i == n - 1))
            return inst
        self.op(self.PE, [ps], ins, fn)

    def act(self, out, in_, func, bias=None, scale=None, eng=None):
        nc = self.nc
        ins = [in_]
        kw = {}
        if bias is not None:
            if isinstance(bias, V):
                ins.append(bias)
                kw["bias"] = bias.ap
            else:
                kw["bias"] = bias
        if scale is not None:
            if isinstance(scale, V):
                ins.append(scale)
                kw["scale"] = scale.ap
            else:
                kw["scale"] = scale
        self.op(self.ACT, [out], ins, lambda: nc.scalar.activation(out=out.ap, in_=in_.ap, func=func, **kw))

    def _sc(self, x, ins):
        if isinstance(x, V):
            ins.append(x)
            return x.ap
        return x

    def ts(self, eng, out, in0, s1, op0, s2=None, op1=None):
        ins = [in0]
        a1 = self._sc(s1, ins)
        a2 = self._sc(s2, ins)
        kw = {} if op1 is None else {"op1": op1}
        self.op(eng, [out], ins, lambda: eng.e.tensor_scalar(out=out.ap, in0=in0.ap, scalar1=a1, scalar2=a2, op0=op0, **kw))

    def stt(self, eng, out, in0, s, in1, op0, op1):
        ins = [in0, in1]
        a = self._sc(s, ins)
        self.op(eng, [out], ins, lambda: eng.e.scalar_tensor_tensor(out=out.ap, in0=in0.ap, scalar=a, in1=in1.ap, op0=op0, op1=op1))

    def tt(self, eng, out, in0, in1, op):
        self.op(eng, [out], [in0, in1], lambda: eng.e.tensor_tensor(out=out.ap, in0=in0.ap, in1=in1.ap, op=op))

    def copy(self, eng, out, in_):
        if eng is self.ACT:
            self.act(out, in_, AF.Identity)
        else:
            self.op(eng, [out], [in_], lambda: eng.e.tensor_copy(out=out.ap, in_=in_.ap))

    def recip(self, out, in_):
        nc = self.nc
        self.op(self.DVE, [out], [in_], lambda: nc.vector.reciprocal(out=out.ap, in_=in_.ap))

    def memset(self, eng, out, val):
        self.op(eng, [out], [], lambda: eng.e.memset(out.ap, val))


def _relayout_w(w):
    Kd, N = w.shape
    assert Kd % 128 == 0 and N % 128 == 0
    return np.ascontiguousarray(w.reshape(Kd // 128, 128, N // 128, 128).transpose(2, 1, 0, 3))


def _cols(v):
    v = np.asarray(v, np.float32).reshape(-1)
    if v.size < 128:
        v = np.tile(v, 128 // v.size)
    return np.ascontiguousarray(v.reshape(-1, 128).T)


class VecPack:
    def __init__(self):
        self.cols = []
        self.idx = {}
        self.n = 0

    def add(self, name, v):
        c = _cols(v)
        self.idx[name] = (self.n, c.shape[1])
        self.cols.append(c)
        self.n += c.shape[1]

    def pack(self):
        return np.ascontiguousarray(np.concatenate(self.cols, axis=1))


def _vec_index():
    vp = VecPack()
    z = np.zeros
    for i in range(DEPTH):
        vp.add(f"n1g{i}", z(D)); vp.add(f"n2g{i}", z(D))
        for k in range(3):
            vp.add(f"fcw{i}_{k}", z(2 * FH))
        vp.add(f"fcb{i}", z(2 * FH))
        vp.add(f"adab{i}", z(6 * D))
    vp.add("mla_qnorm", z(768)); vp.add("mla_kvnorm", z(256))
    vp.add("mla_qn_nope", z(128)); vp.add("mla_qn_rope", z(64))
    vp.add("mla_kn_nope", z(128)); vp.add("mla_kn_rope", z(64))
    vp.add("diff_qn", z(64)); vp.add("diff_kn", z(64)); vp.add("diff_hn", z(128))
    for nm in ("lq1", "lk1", "lq2", "lk2"):
        vp.add("diff_" + nm, z(64))
    for k in range(3):
        vp.add(f"scw{k}", z(D))
    return vp


VIDX = _vec_index()


def _rope_tables(S):
    rows = S // 64
    row = np.repeat(np.arange(rows, dtype=np.float32), 64)
    col = np.tile(np.arange(64, dtype=np.float32), rows)
    n_freq = 16
    inv_freq = (np.float32(10000.0) ** (-np.arange(n_freq, dtype=np.float32) / np.float32(n_freq))).astype(np.float32)
    ang = np.concatenate([row[:, None] * inv_freq, col[:, None] * inv_freq], axis=-1).astype(np.float32)
    cos = np.cos(ang).astype(np.float32).T
    sin = np.sin(ang).astype(np.float32).T
    C = np.concatenate([cos, cos, cos, cos], axis=0)
    Sg = np.concatenate([-sin, sin, -sin, sin], axis=0)
    return np.ascontiguousarray(C), np.ascontiguousarray(Sg)


class Prog:
    def __init__(self, dbg=None):
        self.dbg = dbg
        nc = bass.Bass("TRN2", target_bir_lowering=False)
        self.nc = nc
        self.k = K(nc)
        self.dr = {}
        self.build()

    def din(self, name, shape, dtype=F32):
        t = self.nc.dram_tensor(name, list(shape), dtype, kind="ExternalInput")
        self.dr[name] = t
        return t

    def dout(self, name, shape):
        t = self.nc.dram_tensor(name, list(shape), F32, kind="ExternalOutput")
        self.dr[name] = t
        return t

    def build(self):
        nc, k = self.nc, self.k
        self.din("xT_p", [D, 512]); self.din("xT_s", [D, 2048])
        self.din("condT", [128, 8, 2])
        self.din("vecs", [128, VIDX.n])
        self.din("ropeC", [128, 2048]); self.din("ropeS", [128, 2048])
        self.din("ada_w", [DEPTH, 48, 128, 8, 128])
        self.din("ffn_w_in", [DEPTH, 44, 128, 8, 128]); self.din("ffn_w_out", [DEPTH, 8, 128, 22, 128])
        self.din("mla_w_down", [9, 128, 8, 128]); self.din("mla_w_uq", [8, 128, 6, 192])
        self.din("mla_w_uk", [8, 128, 2, 128]); self.din("mla_w_uv", [8, 128, 2, 128]); self.din("mla_w_o", [8, 128, 8, 128])
        self.din("diff_w_qkv", [24, 128, 8, 128]); self.din("diff_w_o", [8, 128, 8, 128])
        self.din("sconv_w_in", [24, 128, 8, 128]); self.din("sconv_w_out", [8, 128, 8, 128])
        self.din("gmlp_w_in", [16, 128, 8, 128]); self.din("gmlp_w_out", [8, 128, 8, 128])
        self.din("gmlp_wsT", [128, 8, 128]); self.din("gmlp_vn_bc", [128, 1024]); self.din("gmlp_bs_bc", [128, 1024])
        self.din("ctx_ckvT", [256, 256]); self.din("ctx_krT", [64, 256])
        self.din("ctx_dkT", [1024, 256]); self.din("ctx_dv", [256, 1024])
        self.din("c_ident", [128, 128]); self.din("c_perm", [128, 128]); self.din("c_ones64", [128, 128])
        self.dout("yT_p", [D, 512]); self.dout("yT_s", [D, 2048])
        self.dout("st_ckvT", [256, 512]); self.dout("st_krT", [64, 512])
        self.dout("st_dkT", [1024, 512]); self.dout("st_dv", [512, 1024])

        self.vecs = k.alloc("vecs", [128, VIDX.n], F32); self.b_vecs = Buf("vecs")
        self.ident = k.alloc("ident", [128, 128], F32)
        self.perm = k.alloc("perm", [128, 128], F32)
        self.ones_bf = k.alloc("ones_bf", [128, 128], BF16)
        self.perm_bf = k.alloc("perm_bf", [128, 128], BF16)
        self.ones64 = k.alloc("ones64", [128, 128], BF16)
        self.ones_f = k.alloc("ones_f", [128, 128], F32)
        self.b_const = Buf("const")
        self.mod = k.alloc("mod", [128, DEPTH, 48, 2], F32); self.b_mods = [Buf(f"mod{i}") for i in range(DEPTH)]
        self.gs = k.alloc("gs", [128, DEPTH, 2, 2, 8], F32)
        self.lam = k.alloc("lam", [128, 4], F32); self.b_lam = Buf("lam")
        self._eps = k.alloc("epscol", [128, 1], F32)
        k.memset(k.DVE, V(self._eps[:], self.b_const), EPS)
        self._cst = {}
        lam1 = 0.8 - 0.6 * math.exp(-0.3 * 1)
        for val in (EPS, 0.0, math.log(192.0 ** -0.5), math.log(64.0 ** -0.5), math.log(1.0 - lam1)):
            t = k.alloc("cst", [128, 1], F32)
            k.memset(k.DVE, V(t[:], self.b_const), float(val))
            self._cst[round(float(val), 9)] = t
        self.y = k.alloc("y", [128, 8, 2048], F32)
        self.b_y = [Buf(f"y{t}") for t in range(4)]
        self.wslots = [k.alloc(f"wslot{i}", [128, 2048], BF16) for i in range(4)]
        self.b_w = [Buf(f"w{i}") for i in range(4)]
        self.wi = 0
        self.ps = [nc.alloc_psum_tensor(f"ps{i}", [128, 512], F32) for i in range(8)]
        self.b_ps = [Buf(f"ps{i}") for i in range(8)]
        self.psi = 0

        for b in [self.b_vecs, self.b_const] + self.b_y + self.b_w:
            b._ds = k.dsem("own_" + b.name)
        self.consts()
        self.preamble()
        for pi, P in enumerate(self.passes()):
            if self.dbg is not None and self.dbg[1] == "p" and P["sample"]:
                continue
            self.run_pass(P)
        self.finish()

    def vcol(self, name, c=0, rows=128):
        o, n = VIDX.idx[name]
        assert c < n
        return V(self.vecs[0:rows, o + c:o + c + 1], self.b_vecs)

    def psum(self, rows=128, cols=512, ring=None):
        ring = ring or list(range(8))
        key = tuple(ring)
        if not hasattr(self, "_rings"):
            self._rings = {}
        i = self._rings.get(key, 0)
        self._rings[key] = i + 1
        b = ring[i % len(ring)]
        return V(self.ps[b][0:rows, 0:cols], self.b_ps[b])

    def wload(self, parts, total_cols):
        k = self.k
        i = self.wi % 4
        self.wi += 1
        slot, b = self.wslots[i], self.b_w[i]
        assert total_cols <= 2048
        sem = None
        for (o, n, src, shp) in parts:
            dst = slot[:, o:o + n]
            if len(shp) == 2:
                dst = dst.rearrange("p (a b) -> p a b", a=shp[0])
            elif len(shp) == 3:
                dst = dst.rearrange("p (a b c) -> p a b c", a=shp[0], b=shp[1])
            k.dma(k.POOL, dst, src, sem, out_bufs=[b])
        return slot, b

    def wtile(self, dram, n0, nn=1):
        KC, W = dram.shape[2], dram.shape[3]
        src = dram.ap()[n0:n0 + nn].rearrange("n p k j -> p n k j")
        slot, b = self.wload([(0, nn * KC * W, src, (nn, KC, W))], nn * KC * W)
        return V(slot[:, 0:nn * KC * W].rearrange("p (n k j) -> p n k j", n=nn, k=KC), b)

    def consts(self):
        nc, k = self.nc, self.k
        s = None
        k.dma(k.SP, self.vecs[:], self.dr["vecs"].ap(), s, out_bufs=[self.b_vecs])
        bc = [self.b_const]
        k.dma(k.SP, self.ident[:], self.dr["c_ident"].ap(), s, out_bufs=bc)
        k.dma(k.SP, self.perm[:], self.dr["c_perm"].ap(), s, out_bufs=bc)
        k.op(k.DVE, [V(None, bc)], [V(None, bc)], lambda: nc.vector.tensor_copy(out=self.perm_bf[:], in_=self.perm[:]))
        k.op(k.POOL, [V(None, bc)], [], lambda: nc.gpsimd.memset(self.ones64[:], 0.0))
        k.op(k.POOL, [V(None, bc)], [], lambda: nc.gpsimd.memset(self.ones64[0:64, 0:64], 1.0))
        k.op(k.POOL, [V(None, bc)], [], lambda: nc.gpsimd.memset(self.ones64[64:128, 64:128], 1.0))
        k.op(k.POOL, [V(None, bc)], [], lambda: nc.gpsimd.memset(self.ones_bf[:], 1.0))
        k.op(k.POOL, [V(None, bc)], [], lambda: nc.gpsimd.memset(self.ones_f[:], 1.0))

    def cv(self, ap):
        return V(ap, self.b_const)

    def ada_gen(self, i):
        k = self.k
        aw = self.dr["ada_w"].ap()
        o, n = VIDX.idx[f"adab{i}"]
        bm = self.b_mods[i]
        for g in range(6):
            a, ba = self.abuf[self.ada_it % 2], self.b_a[self.ada_it % 2]
            self.ada_it += 1
            src = aw[i, g * 8:(g + 1) * 8].rearrange("n p k j -> p n k j")
            k.dma(k.POOL, a[:], src, None, out_bufs=[ba])
            pv = self.psum(128, 16)
            for n_ in range(8):
                k.mm(V(pv.ap[:, 2 * n_:2 * n_ + 2], pv.bufs),
                     [(V(a[:, n_, kc, :], ba), V(self.cond_b[:, kc, :], self.b_cond)) for kc in range(8)])
            bias = self.vecs[:, o + g * 8:o + (g + 1) * 8]
            k.tt(k.DVE, V(self.mod[:, i, g * 8:(g + 1) * 8, :], bm),
                 V(pv.ap.rearrange("p (c t) -> p c t", t=2), pv.bufs),
                 V(bias.unsqueeze(2).to_broadcast([128, 8, 2]), self.b_vecs), ALU.add)
            yield
        for w, (gname, sc0) in enumerate(((f"n1g{i}", 8), (f"n2g{i}", 32))):
            go, _ = VIDX.idx[gname]
            for cd in range(2):
                k.stt(k.DVE, V(self.gs[:, i, w, cd, :], bm), V(self.mod[:, i, sc0:sc0 + 8, cd], bm), 1.0,
                      V(self.vecs[:, go:go + 8], self.b_vecs), ALU.add, ALU.mult)
        yield

    def preamble(self):
        nc, k = self.nc, self.k
        self.ada_m0 = k.mark()
        cond_f = k.alloc("cond_f", [128, 8, 2], F32)
        self.cond_b = k.alloc("cond_b", [128, 8, 2], BF16)
        self.b_cond = Buf("cond")
        b_c = self.b_cond
        self.abuf = [k.alloc(f"adaw{i}", [128, 8, 8, 128], BF16) for i in range(2)]
        self.b_a = [Buf(f"adaw{i}") for i in range(2)]
        self.ada_it = 0
        k.dma(k.SP, cond_f[:], self.dr["condT"].ap(), None, out_bufs=[b_c])
        k.act(V(self.cond_b[:], b_c), V(cond_f[:], b_c), AF.Silu)
        self.drain(self.ada_gen(0))
        pr = k.alloc("lam_pr", [128, 2], F32)
        b_l = Buf("lamtmp")
        for j, (a, b) in enumerate((("diff_lq1", "diff_lk1"), ("diff_lq2", "diff_lk2"))):
            k.tt(k.DVE, V(pr[0:64, j:j + 1], b_l), self.vcol(a, 0, 64), self.vcol(b, 0, 64), ALU.mult)
        pl = self.psum(128, 2)
        k.mm(pl, [(self.cv(self.ones_f[0:64, :]), V(pr[0:64, :], b_l))])
        ex = k.alloc("lam_ex", [128, 2], F32)
        k.act(V(ex[:], b_l), pl, AF.Exp)
        lam_init = 0.8 - 0.6 * math.exp(-0.3 * 1)
        k.stt(k.DVE, V(self.lam[:, 0:1], self.b_lam), V(ex[:, 0:1], b_l), lam_init, V(ex[:, 1:2], b_l), ALU.add, ALU.subtract)
        k.ts(k.DVE, V(self.lam[:, 1:2], self.b_lam), V(self.lam[:, 0:1], self.b_lam), -1.0, ALU.mult)
        k.barrier()

    def passes(self):
        return [
            dict(name="p", T=512, NT=1, seqs=[(0, 256), (256, 256)], cond=0, sample=False,
                 xin=self.dr["xT_p"], yout=self.dr["yT_p"], QT=256),
            dict(name="s", T=2048, NT=4, seqs=[(0, 2048)], cond=1, sample=True,
                 xin=self.dr["xT_s"], yout=self.dr["yT_s"], QT=512),
        ]

    def yv(self, tt, c=None):
        if c is None:
            return V(self.y[:, :, tt * 512:(tt + 1) * 512], self.b_y[tt])
        return V(self.y[:, c, tt * 512:(tt + 1) * 512], self.b_y[tt])

    def modcol(self, i, which, cond, c):
        return V(self.mod[:, i, which * 8 + c, cond:cond + 1], self.b_mods[i])

    def run_pass(self, P):
        nc, k = self.nc, self.k
        T, NT = P["T"], P["NT"]
        xin = P["xin"].ap().rearrange("(c p) t -> p c t", p=128)
        for tt in range(NT):
            k.dma(k.SP, self.y[:, :, tt * 512:(tt + 1) * 512], xin[:, :, tt * 512:(tt + 1) * 512], None,
                  out_bufs=[self.b_y[tt]])
        stop = self.dbg
        for i in range(DEPTH):
            if stop is not None and stop[0] < 0:
                break
            m0 = k.mark()
            [self.mla, self.diff, self.sconv, self.gmlp][i % 4](i, P)
            k.barrier()
            k.release(m0)
            if stop == (i, "m"):
                break
            m0 = k.mark()
            side = None
            if not P["sample"] and i + 1 < DEPTH:
                side = self.ada_gen(i + 1)
            self.ffn(i, P, side=side)
            self.drain(side)
            k.barrier()
            k.release(m0)
            if stop == (i, "f"):
                break
        yout = P["yout"].ap().rearrange("(c p) t -> p c t", p=128)
        for tt in range(NT):
            k.dma(k.SP, yout[:, :, tt * 512:(tt + 1) * 512], self.y[:, :, tt * 512:(tt + 1) * 512], None,
                  in_bufs=[self.b_y[tt]])
        k.barrier()
        if not P["sample"]:
            k.release(self.ada_m0)

    def finish(self):
        k = self.k
        for s in k.dsems.values():
            if s.n and k.SP.seen.get(s, 0) < s.n:
                k.SP.e.wait_ge(s.h, s.n)
                k.SP.seen[s] = s.n

    def normmod_g(self, i, which, P, tt, h_out, scr):
        k = self.k
        cd = P["cond"]
        sq, rs, tmp, b_s = scr["sq"], scr["rs"], scr["tmp"], scr["b"]
        yv = self.yv(tt)
        k.act(V(sq[:], b_s), yv, AF.Square)
        yield
        pv = self.psum()
        k.mm(pv, [(self.cv(self.ones_bf[:]), V(sq[:, c, :], b_s)) for c in range(8)])
        yield
        k.act(V(rs[:], b_s), pv, AF.Ln, bias=self.cst(EPS), scale=1.0 / D)
        yield
        k.act(V(rs[:], b_s), V(rs[:], b_s), AF.Exp, scale=-0.5)
        yield
        for c in range(8):
            tv = V(tmp[:, c % 2, :], scr["bt"][c % 2])
            k.stt(k.DVE, tv, self.yv(tt, c), V(self.gs[:, i, which, cd, c:c + 1], self.b_mods[i]), V(rs[:], b_s), ALU.mult, ALU.mult)
            k.act(V(h_out.ap[:, c, :], h_out.bufs), tv, AF.Identity, bias=self.modcol(i, 3 * which, cd, c))
            if c % 2 == 1:
                yield

    def normmod(self, *a):
        self.drain(self.normmod_g(*a))

    def normmod_all(self, i, which, P, h, b_h):
        NT = P["NT"]
        scrs = [self.nm_scratch() for _ in range(min(2, NT))]
        self.drain(self.rr([self.normmod_g(i, which, P, tt, V(h[:, :, tt * 512:(tt + 1) * 512], b_h[tt]), scrs[tt % len(scrs)])
                            for tt in range(NT)], len(scrs)))

    def eps_col(self):
        return V(self._eps[:], self.b_const)

    def nm_scratch(self):
        k = self.k
        return dict(sq=k.alloc("nm_sq", [128, 8, 512], BF16), rs=k.alloc("nm_rs", [128, 512], F32),
                    tmp=k.alloc("nm_tmp", [128, 2, 512], F32), b=Buf("nm"), bt=[Buf("nmt0"), Buf("nmt1")])

    def resid(self, i, which, P, tt, m, pv, eng=None):
        k = self.k
        yv = self.yv(tt, m)
        k.stt(eng or k.DVE, yv, pv, self.modcol(i, 3 * which + 2, P["cond"], m), yv, ALU.mult, ALU.add)

    def out_proj(self, i, P, wdram, src, b_src, KC, k0=0):
        k = self.k
        for m in range(8):
            if KC == wdram.shape[2]:
                w = self.wtile(wdram, m)
                wap = w.ap[:, 0]
            else:
                srcw = wdram.ap()[m, :, k0:k0 + KC, :]
                slot, bw = self.wload([(0, KC * 128, srcw, (KC, 128))], KC * 128)
                w = V(slot[:, 0:KC * 128].rearrange("p (k j) -> p k j", k=KC), bw)
                wap = w.ap
            for tt in range(P["NT"]):
                pv = self.psum(ring=[5, 6, 7])
                k.mm(pv, [(V(wap[:, kc, :], w.bufs), V(src[:, kc, tt * 512:(tt + 1) * 512], b_src)) for kc in range(KC)])
                self.resid(i, 0, P, tt, m, pv)

    def ffn(self, i, P, side=None):
        k = self.k
        T, NT, seqs = P["T"], P["NT"], P["seqs"]
        Q = 2
        JJ = 22 // Q
        h = k.alloc("ffn_h", [128, 8, T], BF16); b_h = [Buf(f"fh{t}") for t in range(NT)]
        act = k.alloc("ffn_act", [128, JJ, T], BF16); b_act = Buf("fact")
        m1 = k.mark()
        self.normmod_all(i, 1, P, h, b_h)
        k.barrier()
        k.release(m1)
        npad = T + 2 * len(seqs)
        araw = [k.alloc(f"ffn_araw{g}", [128, npad], F32) for g in range(2)]
        b_ar = [Buf("arg"), Buf("aru")]
        cbuf = [k.alloc(f"ffn_c{g}", [128, T], F32) for g in range(2)]
        b_c = [Buf("cg"), Buf("cu")]
        for g in range(2):
            k.memset(k.POOL, V(araw[g][:], b_ar[g]), 0.0)
        win = self.dr["ffn_w_in"].ap()
        wout = self.dr["ffn_w_out"].ap()
        for q in range(Q):
            for jj in range(JJ):
                j = q * JJ + jj
                if side is not None and j % 3 == 1:
                    next(side, None)
                srcs = [win[i, j], win[i, 22 + j]]
                slot, bw = self.wload([(g * 1024, 1024, srcs[g], (8, 128)) for g in range(2)], 2048)
                wv = slot[:, 0:2048].rearrange("p (g k j) -> p g k j", g=2, k=8)
                for g in range(2):
                    for tt in range(NT):
                        pv = self.psum()
                        k.mm(pv, [(V(wv[:, g, kc, :], bw), V(h[:, kc, tt * 512:(tt + 1) * 512], b_h[tt])) for kc in range(8)])
                        for si, (s0, S) in enumerate(seqs):
                            lo, hi = max(s0, tt * 512), min(s0 + S, (tt + 1) * 512)
                            if lo >= hi:
                                continue
                            base = si * (S + 2) + 1 + (lo - s0)
                            k.copy(k.ACT, V(araw[g][:, base:base + hi - lo], b_ar[g]),
                                   V(pv.ap[:, lo - tt * 512:hi - tt * 512], pv.bufs))
                    col = j if g == 0 else 22 + j
                    for si, (s0, S) in enumerate(seqs):
                        base = si * (S + 2)
                        cv = V(cbuf[g][:, s0:s0 + S], b_c[g])
                        k.act(cv, V(araw[g][:, base + 1:base + 1 + S], b_ar[g]), AF.Identity,
                              bias=self.vcol(f"fcb{i}", col), scale=self.vcol(f"fcw{i}_1", col))
                        k.stt(k.DVE, cv, V(araw[g][:, base:base + S], b_ar[g]), self.vcol(f"fcw{i}_0", col), cv, ALU.mult, ALU.add)
                        k.stt(k.DVE, cv, V(araw[g][:, base + 2:base + 2 + S], b_ar[g]), self.vcol(f"fcw{i}_2", col), cv, ALU.mult, ALU.add)
                k.act(V(cbuf[0][:], b_c[0]), V(cbuf[0][:], b_c[0]), AF.Silu)
                k.tt(k.DVE, V(act[:, jj, :], b_act), V(cbuf[0][:], b_c[0]), V(cbuf[1][:], b_c[1]), ALU.mult)
            for m in range(8):
                src = wout[i, m, :, q * JJ:(q + 1) * JJ, :]
                slot, bw = self.wload([(0, JJ * 128, src, (JJ, 128))], JJ * 128)
                wv = slot[:, 0:JJ * 128].rearrange("p (k j) -> p k j", k=JJ)
                for tt in range(NT):
                    pv = self.psum()
                    k.mm(pv, [(V(wv[:, jj, :], bw), V(act[:, jj, tt * 512:(tt + 1) * 512], b_act)) for jj in range(JJ)])
                    self.resid(i, 1, P, tt, m, pv)

    def sconv(self, i, P):
        k = self.k
        T, NT, seqs = P["T"], P["NT"], P["seqs"]
        h = k.alloc("sc_h", [128, 8, T], BF16); b_h = [Buf(f"sh{t}") for t in range(NT)]
        g = k.alloc("sc_g", [128, 8, T], BF16); b_g = Buf("scg")
        npad = T + 2 * len(seqs)
        tpad = k.alloc("sc_t", [128, npad], F32); b_t = Buf("sct")
        gb = k.alloc("sc_gb", [128, T], F32); b_gb = Buf("scgb")
        gc = k.alloc("sc_gc", [128, 512], F32); b_gc = Buf("scgc")
        cv_t = k.alloc("sc_cv", [128, T], F32); b_cv = Buf("sccv")
        m1 = k.mark()
        self.normmod_all(i, 0, P, h, b_h)
        k.barrier()
        k.release(m1)
        k.memset(k.POOL, V(tpad[:], b_t), 0.0)
        for c in range(8):
            wts = [self.wtile(self.dr["sconv_w_in"], a * 8 + c) for a in range(3)]
            for tt in range(NT):
                sl = slice(tt * 512, (tt + 1) * 512)
                pvs = []
                for a in range(3):
                    pv = self.psum()
                    k.mm(pv, [(V(wts[a].ap[:, 0, kc, :], wts[a].bufs), V(h[:, kc, sl], b_h[tt])) for kc in range(8)])
                    pvs.append(pv)
                k.copy(k.ACT, V(gb[:, sl], b_gb), pvs[0])
                k.copy(k.ACT, V(gc[:], b_gc), pvs[1])
                for si, (s0, S) in enumerate(seqs):
                    lo, hi = max(s0, tt * 512), min(s0 + S, (tt + 1) * 512)
                    if lo >= hi:
                        continue
                    base = si * (S + 2) + 1 + (lo - s0)
                    k.tt(k.DVE, V(tpad[:, base:base + hi - lo], b_t), V(pvs[2].ap[:, lo - tt * 512:hi - tt * 512], pvs[2].bufs),
                         V(gc[:, lo - tt * 512:hi - tt * 512], b_gc), ALU.mult)
            for si, (s0, S) in enumerate(seqs):
                base = si * (S + 2)
                cv = V(cv_t[:, s0:s0 + S], b_cv)
                k.ts(k.DVE, cv, V(tpad[:, base + 1:base + 1 + S], b_t), self.vcol("scw1", c), ALU.mult)
                k.stt(k.DVE, cv, V(tpad[:, base:base + S], b_t), self.vcol("scw0", c), cv, ALU.mult, ALU.add)
                k.stt(k.DVE, cv, V(tpad[:, base + 2:base + 2 + S], b_t), self.vcol("scw2", c), cv, ALU.mult, ALU.add)
            k.tt(k.DVE, V(g[:, c, :], b_g), V(cv_t[:], b_cv), V(gb[:], b_gb), ALU.mult)
        self.out_proj(i, P, self.dr["sconv_w_out"], g, b_g, 8)


    S_RING = [0, 1]
    SIDE_RING = [2, 5, 6, 7]

    def cst(self, val, rows=128):
        key = round(float(val), 9)
        return V(self._cst[key][0:rows, :], self.b_const)

    @staticmethod
    def drain(g):
        if g is not None:
            for _ in g:
                pass

    @staticmethod
    def rr(chains, window):
        it = iter(chains)
        active = []
        while True:
            while len(active) < window:
                g = next(it, None)
                if g is None:
                    break
                active.append(g)
            if not active:
                return
            for g in list(active):
                try:
                    next(g)
                except StopIteration:
                    active.remove(g)
                yield

    def rms_g(self, src, nch, rows, dim, gain_fn, outs, scr, ones=None, extra_scale=1.0, ring=None):
        k = self.k
        sq, rs, b = scr
        N = src[0].ap.shape[-1]
        for c in range(nch):
            k.act(V(sq[0:rows, c, 0:N], b), src[c], AF.Square)
        yield
        pv = self.psum(rows, N, ring=ring or self.SIDE_RING)
        on = ones if ones is not None else self.ones_bf
        k.mm(pv, [(self.cv(on[0:rows, 0:rows]), V(sq[0:rows, c, 0:N], b)) for c in range(nch)])
        yield
        rsv = V(rs[0:rows, 0:N], b)
        k.act(rsv, pv, AF.Ln, bias=self.cst(EPS, rows), scale=1.0 / dim)
        yield
        k.act(rsv, rsv, AF.Exp, bias=self.cst(math.log(extra_scale), rows), scale=-0.5)
        yield
        for c in range(nch):
            k.stt(k.DVE, outs[c], src[c], gain_fn(c), rsv, ALU.mult, ALU.mult)
        yield

    def rms_fm(self, *a, **kw):
        self.drain(self.rms_g(*a, **kw))

    def rope_g(self, x, out, rows, t0, N, scr, ring=None):
        k = self.k
        t1, b = scr
        if not hasattr(self, "_xb"):
            self._xb = {}
        key = id(t1)
        xb = self._xb[key]
        k.copy(k.ACT, V(xb[0:rows, 0:N], b), x)
        yield
        pv = self.psum(rows, N, ring=ring or self.SIDE_RING)
        k.mm(pv, [(self.cv(self.perm_bf[0:rows, 0:rows]), V(xb[0:rows, 0:N], b))])
        yield
        k.tt(k.DVE, V(t1[0:rows, 0:N], b), pv, V(self.ropeS[0:rows, t0:t0 + N], self.b_rope), ALU.mult)
        k.tt(k.DVE, x, x, V(self.ropeC[0:rows, t0:t0 + N], self.b_rope), ALU.mult)
        yield
        if isinstance(out, list):
            for (r0, r1, ov) in out:
                k.tt(k.DVE, ov, V(x.ap[r0:r1], x.bufs), V(t1[r0:r1, 0:N], b), ALU.add)
        else:
            k.tt(k.DVE, out, x, V(t1[0:rows, 0:N], b), ALU.add)
        yield

    def rope_fm(self, *a, **kw):
        self.drain(self.rope_g(*a, **kw))

    def load_rope(self):
        k = self.k
        self.ropeC = k.alloc("ropeC", [128, 2048], F32)
        self.ropeS = k.alloc("ropeS", [128, 2048], F32)
        self.b_rope = Buf("rope")
        k.dma(k.SP, self.ropeC[:], self.dr["ropeC"].ap(), None, out_bufs=[self.b_rope])
        k.dma(k.SP, self.ropeS[:], self.dr["ropeS"].ap(), None, out_bufs=[self.b_rope])

    def attend(self, nk, QT, score_pairs, v_lhsT, pts, b_pts, out, rl_scr, side=None, spi=1):
        k = self.k
        po = V(self.ps[3][:, 0:QT], self.b_ps[3])
        pl = V(self.ps[4][:, 0:QT], self.b_ps[4])
        pss = [None] * nk
        pss[0] = self.psum(128, QT, ring=self.S_RING)
        k.mm(pss[0], score_pairs(0))
        for kc in range(nk):
            if kc + 1 < nk:
                pss[kc + 1] = self.psum(128, QT, ring=self.S_RING)
                k.mm(pss[kc + 1], score_pairs(kc + 1))
            pt = V(pts[kc % 3][:, 0:QT], b_pts[kc % 3])
            k.act(pt, pss[kc], AF.Exp)
            self.mm1(po, v_lhsT(kc), pt, kc == 0, kc == nk - 1)
            self.mm1(pl, self.cv(self.ones_bf[:]), pt, kc == 0, kc == nk - 1)
            if side is not None:
                for _ in range(spi):
                    if next(side, "done") == "done":
                        side = None
                        break
        rl, b_rl = rl_scr
        rlv = V(rl[:, 0:QT], b_rl)
        k.act(rlv, pl, AF.Ln)
        k.act(rlv, rlv, AF.Exp, scale=-1.0)
        k.tt(k.DVE, out, po, rlv, ALU.mult)
        return side

    def mm1(self, ps, l, r, start, stop):
        nc = self.nc
        self.k.op(self.k.PE, [ps], [l, r], lambda: nc.tensor.matmul(ps.ap, lhsT=l.ap, rhs=r.ap, start=start, stop=stop))

    def store(self, dram_ap, src_ap, bufs, name):
        k = self.k
        k.dma(k.SP, dram_ap, src_ap, None, in_bufs=bufs)

    def mla(self, i, P):
        k, nc = self.k, self.nc
        T, NT, seqs, sample, QT = P["T"], P["NT"], P["seqs"], P["sample"], P["QT"]
        CTX = 256 if sample else 0
        LT = T + CTX
        scale = 192.0 ** -0.5
        cqn = k.alloc("ml_cqn", [128, 6, T], BF16); b_cqn = Buf("cqn")
        ckvn = k.alloc("ml_ckvn", [128, 2, LT], BF16); b_ckvn = Buf("ckvn")
        krall = k.alloc("ml_kr", [128, LT], BF16); b_kr = Buf("krall")
        k.memset(k.POOL, V(krall[64:128, :], b_kr), 0.0)
        if sample:
            self.load_rope()
            k.dma(k.POOL, ckvn[:, :, 0:256], self.dr["ctx_ckvT"].ap().rearrange("(c p) t -> p c t", p=128), None, out_bufs=[b_ckvn])
            k.dma(k.POOL, krall[0:64, 0:256], self.dr["ctx_krT"].ap(), None, out_bufs=[b_kr])
        m1 = k.mark()
        scr = self.nm_scratch()
        ht = k.alloc("ml_ht", [128, 8, 512], BF16); b_ht = Buf("ht")
        dn = k.alloc("ml_dn", [128, 9, 512], F32); b_dn = Buf("dn")
        sqs = [(k.alloc(f"ml_sq{j}", [128, nchs, 512], BF16), k.alloc(f"ml_rs{j}", [128, 512], F32), Buf(f"mlsq{j}"))
               for j, nchs in enumerate((6, 2, 1))]
        t1 = k.alloc("ml_t1", [128, 512], F32); b_t1 = Buf("mlt1")
        self._xb = getattr(self, "_xb", {}); self._xb[id(t1)] = k.alloc("ml_xb", [128, 512], BF16)
        stf = k.alloc("ml_stf", [128, 3, 512], F32); b_stf = Buf("stf")
        wd = self.dr["mla_w_down"]
        for tt in range(NT):
            sl = slice(tt * 512, (tt + 1) * 512)
            self.normmod(i, 0, P, tt, V(ht[:], b_ht), scr)
            for n in range(9):
                rows = 128 if n < 8 else 64
                w = self.wtile(wd, n)
                pv = self.psum(rows, 512, ring=[5, 6, 7])
                k.mm(pv, [(V(w.ap[:, 0, kc, 0:rows], w.bufs), V(ht[:, kc, :], b_ht)) for kc in range(8)])
                k.copy(k.ACT if n % 2 else k.DVE, V(dn[0:rows, n, :], b_dn), pv)
            if sample:
                outs = [V(ckvn[:, c, CTX + tt * 512:CTX + (tt + 1) * 512], b_ckvn) for c in range(2)]
            else:
                outs = [V(stf[:, c, :], b_stf) for c in range(2)]
            krf = V(stf[0:64, 2, :], b_stf)

            def kr_chain(tt=tt, krf=krf):
                yield from self.rms_g([V(dn[0:64, 8, :], b_dn)], 1, 64, 64, lambda c: self.vcol("mla_kn_rope", 0, 64), [krf], sqs[2],
                                      ring=[0, 1, 2, 3, 4])
                if sample:
                    yield from self.rope_g(krf, V(krall[0:64, CTX + tt * 512:CTX + (tt + 1) * 512], b_kr), 64, tt * 512, 512, (t1, b_t1),
                                           ring=[0, 1, 2, 3, 4])
            chains = [
                self.rms_g([V(dn[:, c, :], b_dn) for c in range(6)], 6, 128, 768, lambda c: self.vcol("mla_qnorm", c),
                           [V(cqn[:, c, sl], b_cqn) for c in range(6)], sqs[0], ring=[0, 1, 2, 3, 4]),
                self.rms_g([V(dn[:, 6 + c, :], b_dn) for c in range(2)], 2, 128, 256, lambda c: self.vcol("mla_kvnorm", c),
                           outs, sqs[1], ring=[0, 1, 2, 3, 4]),
                kr_chain(),
            ]
            self.drain(self.rr(chains, 3))
            if not sample:
                for c in range(2):
                    k.copy(k.ACT, V(ckvn[:, c, sl], b_ckvn), V(stf[:, c, :], b_stf))
                k.copy(k.ACT, V(krall[0:64, sl], b_kr), krf)
                self.store(self.dr["st_ckvT"].ap().rearrange("(c p) t -> p c t", p=128)[:, :, sl], stf[:, 0:2, :], [b_stf], "st")
                self.store(self.dr["st_krT"].ap()[:, sl], stf[0:64, 2, :], [b_stf], "st")
        k.barrier()
        k.release(m1)
        LC = LT // 128
        oT = k.alloc("ml_oT", [128, 2, T], BF16); b_oT = Buf("oT")
        knh = [k.alloc(f"ml_knh{j}", [128, LT], BF16) for j in range(2)]; b_knh = [Buf(f"knh{j}") for j in range(2)]
        vh = [k.alloc(f"ml_vh{j}", [128, LC, 128], BF16) for j in range(2)]; b_vh = [Buf(f"vh{j}") for j in range(2)]
        wuqs = [k.alloc(f"ml_wuq{j}", [128, 6, 192], BF16) for j in range(2)]; b_wuq = [Buf(f"wuq{j}") for j in range(2)]
        qn = [k.alloc(f"ml_qn{j}", [128, 512], BF16) for j in range(2)]
        qr = [k.alloc(f"ml_qr{j}", [128, 512], BF16) for j in range(2)]
        qrf = [k.alloc(f"ml_qrf{j}", [64, 512], F32) for j in range(2)]
        b_q = [Buf(f"mlq{j}") for j in range(2)]
        for j in range(2):
            k.memset(k.POOL, V(qr[j][64:128, :], b_q[j]), 0.0)
        sqs = [(k.alloc(f"ml_sqb{j}", [128, 1, 512], BF16), k.alloc(f"ml_rsb{j}", [128, 512], F32), Buf(f"mlsqb{j}")) for j in range(3)]
        t1 = k.alloc("ml_t12", [128, 512], F32); b_t1 = Buf("mlt12")
        self._xb = getattr(self, "_xb", {}); self._xb[id(t1)] = k.alloc("ml_xb2", [128, 512], BF16)
        pts = [k.alloc(f"ml_pt{j}", [128, 512], BF16) for j in range(3)]; b_pts = [Buf(f"pt{j}") for j in range(3)]
        rl = k.alloc("ml_rl", [128, 512], F32); b_rl = Buf("rl")
        items = [(hd, s0, S, q0) for hd in range(8) for (s0, S) in seqs for q0 in range(s0, s0 + S, QT)]

        def kprep(hd):
            sl = hd % 2
            wuk = self.wtile(self.dr["mla_w_uk"], hd)
            wuv = self.wtile(self.dr["mla_w_uv"], hd)
            k.dma(k.POOL, wuqs[sl][:], self.dr["mla_w_uq"].ap()[hd], None, out_bufs=[b_wuq[sl]])

            def kn_chain(c0, ci):
                n = min(512, LT - c0)
                pv = self.psum(128, n, ring=self.SIDE_RING)
                k.mm(pv, [(V(wuk.ap[:, 0, kc, :], wuk.bufs), V(ckvn[:, kc, c0:c0 + n], b_ckvn)) for kc in range(2)])
                yield
                yield from self.rms_g([pv], 1, 128, 128, lambda c: self.vcol("mla_kn_nope"), [V(knh[sl][:, c0:c0 + n], b_knh[sl])],
                                      sqs[ci % 2])

            def v_chain(c0):
                nn = min(4, LC - c0)
                pv = self.psum(128, nn * 128, ring=self.SIDE_RING)
                for j in range(nn):
                    k.mm(V(pv.ap[:, j * 128:(j + 1) * 128], pv.bufs),
                         [(V(ckvn[:, kc, (c0 + j) * 128:(c0 + j + 1) * 128], b_ckvn), V(wuv.ap[:, 0, kc, :], wuv.bufs)) for kc in range(2)])
                yield
                k.copy(k.ACT, V(vh[sl][:, c0:c0 + nn, :], b_vh[sl]), V(pv.ap.rearrange("p (a b) -> p a b", b=128), pv.bufs))
                yield
            chains = [kn_chain(c0, ci) for ci, c0 in enumerate(range(0, LT, 512))] + [v_chain(c0) for c0 in range(0, LC, 4)]
            yield from self.rr(chains, 2)

        def qprep(n):
            hd, s0, S, q0 = items[n]
            sl, ws = n % 2, hd % 2
            wq, bq = wuqs[ws], b_wuq[ws]
            pvn = self.psum(128, QT, ring=self.SIDE_RING)
            k.mm(pvn, [(V(wq[:, kc, 0:128], bq), V(cqn[:, kc, q0:q0 + QT], b_cqn)) for kc in range(6)])
            yield
            yield from self.rms_g([pvn], 1, 128, 128, lambda c: self.vcol("mla_qn_nope"), [V(qn[sl][:, 0:QT], b_q[sl])], sqs[2],
                                  extra_scale=scale)
            pvr = self.psum(64, QT, ring=self.SIDE_RING)
            k.mm(pvr, [(V(wq[:, kc, 128:192], bq), V(cqn[:, kc, q0:q0 + QT], b_cqn)) for kc in range(6)])
            yield
            if sample:
                qf = V(qrf[sl][:, 0:QT], b_q[sl])
                yield from self.rms_g([pvr], 1, 64, 64, lambda c: self.vcol("mla_qn_rope", 0, 64), [qf], sqs[2], extra_scale=scale)
                yield from self.rope_g(qf, V(qr[sl][0:64, 0:QT], b_q[sl]), 64, q0, QT, (t1, b_t1))
            else:
                yield from self.rms_g([pvr], 1, 64, 64, lambda c: self.vcol("mla_qn_rope", 0, 64), [V(qr[sl][0:64, 0:QT], b_q[sl])], sqs[2],
                                      extra_scale=scale)

        def prep(n):
            hd = items[n][0]
            if n == 0 or items[n - 1][0] != hd:
                yield from kprep(hd)
            yield from qprep(n)

        self.drain(prep(0))
        for n, (hd, s0, S, q0) in enumerate(items):
            sl, ks = n % 2, hd % 2
            kcols = list(range(0, CTX, 128)) + [CTX + s0 + a for a in range(0, S, 128)]
            side, spi = None, 1
            if n + 1 < len(items):
                side = prep(n + 1)
                spi = 4 if items[n + 1][0] != hd else 1

            def sp(kc, kcols=kcols, sl=sl, ks=ks):
                c0 = kcols[kc]
                return [(V(knh[ks][:, c0:c0 + 128], b_knh[ks]), V(qn[sl][:, 0:QT], b_q[sl])),
                        (V(krall[:, c0:c0 + 128], b_kr), V(qr[sl][:, 0:QT], b_q[sl]))]
            side = self.attend(len(kcols), QT, sp, lambda kc, kcols=kcols, ks=ks: V(vh[ks][:, kcols[kc] // 128, :], b_vh[ks]),
                               pts, b_pts, V(oT[:, hd % 2, q0:q0 + QT], b_oT), (rl, b_rl), side=side, spi=spi)
            self.drain(side)
            last_of_head = (n + 1 == len(items)) or items[n + 1][0] != hd
            if last_of_head and hd % 2 == 1:
                self.out_proj(i, P, self.dr["mla_w_o"], oT, b_oT, 2, k0=hd - 1)

    def diff(self, i, P):
        k, nc = self.k, self.nc
        T, NT, seqs, sample, QT = P["T"], P["NT"], P["seqs"], P["sample"], P["QT"]
        CTX = 256 if sample else 0
        LT = T + CTX
        LC = LT // 128
        scale = 64.0 ** -0.5
        lam_init = 0.8 - 0.6 * math.exp(-0.3 * i)
        h = k.alloc("df_h", [128, 8, T], BF16); b_h = [Buf(f"dh{t}") for t in range(NT)]
        oT = k.alloc("df_oT", [128, 2, T], BF16); b_oT = Buf("doT")
        if sample:
            self.load_rope()
        m1 = k.mark()
        self.normmod_all(i, 0, P, h, b_h)
        k.barrier()
        k.release(m1)
        qP = k.alloc("df_q", [128, 2, 2, T], BF16); b_q = [Buf("dq0"), Buf("dq1")]
        k.memset(k.POOL, V(qP[64:128, 0], b_q[0]), 0.0)
        k.memset(k.POOL, V(qP[0:64, 1], b_q[1]), 0.0)
        kT = k.alloc("df_k", [128, 2, LT], BF16); b_k = [Buf("dk0"), Buf("dk1")]
        vv = k.alloc("df_v", [128, LC, 256], BF16); b_v = Buf("dv")
        W = 2
        scrs = [dict(xf=k.alloc(f"df_xf{j}", [128, 512], F32), b_xf=Buf(f"dxf{j}"),
                     sq=(k.alloc(f"df_sq{j}", [128, 1, 512], BF16), k.alloc(f"df_rs{j}", [128, 512], F32), Buf(f"dsq{j}")),
                     t1=(k.alloc(f"df_t1{j}", [128, 512], F32), Buf(f"dt1{j}"))) for j in range(W)]
        self._xb = getattr(self, "_xb", {})
        for S_ in scrs:
            self._xb[id(S_["t1"][0])] = k.alloc("df_xb", [128, 512], BF16)
        pts = [k.alloc(f"df_pt{j}", [128, 512], BF16) for j in range(3)]; b_pts = [Buf(f"dpt{j}") for j in range(3)]
        rl = k.alloc("df_rl", [128, 512], F32); b_rl = Buf("drl")
        o12 = [[k.alloc(f"df_o{a}{m}", [128, 512], F32) for m in range(2)] for a in range(2)]
        b_o = [Buf("do0"), Buf("do1")]
        vst = k.alloc("df_vst", [128, 256], F32) if not sample else None
        b_vst = Buf("dvst")
        wq = self.dr["diff_w_qkv"]
        dkT = self.dr["ctx_dkT"].ap().rearrange("(c p) t -> p c t", p=128)
        stdk = self.dr["st_dkT"].ap().rearrange("(c p) t -> p c t", p=128)
        for j in range(4):
            if sample:
                for mp in range(2):
                    k.dma(k.POOL, kT[:, mp, 0:256], dkT[:, mp * 4 + j, :], None, out_bufs=[b_k[mp]])
                k.dma(k.POOL, vv[:, 0:2, :], self.dr["ctx_dv"].ap().rearrange("(c p) f -> p c f", p=128)[:, :, j * 256:(j + 1) * 256],
                      None, out_bufs=[b_v])
            wcache = {}

            def getw(key, n0, nn=1):
                if key not in wcache:
                    wcache[key] = self.wtile(wq, n0, nn)
                return wcache[key]

            def qk_chain(which, mp, tt, ci):
                S_ = scrs[ci % W]
                dst, b_dst, off, gname = ((None, b_q[mp], 0, "diff_qn"), (kT, b_k[mp], CTX, "diff_kn"))[which]
                w = getw((which, mp), which * 8 + mp * 4 + j)
                sl = slice(tt * 512, (tt + 1) * 512)
                pv = self.psum(128, 512, ring=[0, 1, 2, 3])
                k.mm(pv, [(V(w.ap[:, 0, kc, :], w.bufs), V(h[:, kc, sl], b_h[tt])) for kc in range(8)])
                yield
                dsl = slice(off + tt * 512, off + (tt + 1) * 512)
                es = scale if which == 0 else 1.0
                gf = lambda c, g=gname: self.vcol(g)
                xv = V(S_["xf"][:], S_["b_xf"])
                qouts = None
                if which == 0:
                    qouts = [(0, 64, V(qP[0:64, 0, mp, dsl], b_q[mp])), (64, 128, V(qP[64:128, 1, mp, dsl], b_q[mp]))]
                if sample:
                    yield from self.rms_g([pv], 1, 128, 64, gf, [xv], S_["sq"], ones=self.ones64, extra_scale=es, ring=[4, 5, 6, 7])
                    yield from self.rope_g(xv, qouts if which == 0 else V(dst[:, mp, dsl], b_dst), 128, tt * 512, 512, S_["t1"],
                                           ring=[4, 5, 6, 7])
                elif which == 0:
                    yield from self.rms_g([pv], 1, 128, 64, gf, [xv], S_["sq"], ones=self.ones64, extra_scale=es, ring=[4, 5, 6, 7])
                    for (r0, r1, ov) in qouts:
                        k.copy(k.ACT, ov, V(S_["xf"][r0:r1, :], S_["b_xf"]))
                    yield
                else:
                    yield from self.rms_g([pv], 1, 128, 64, gf, [xv], S_["sq"], ones=self.ones64, ring=[4, 5, 6, 7])
                    k.copy(k.ACT, V(dst[:, mp, dsl], b_dst), xv)
                    self.store(stdk[:, mp * 4 + j, sl], S_["xf"][:], [S_["b_xf"]], "st")
                    yield

            def v_chain(a):
                tt = a // 4
                w = getw("v", 16 + 2 * j, 2)
                pv = self.psum(128, 256, ring=[0, 1, 2, 3])
                k.mm(V(pv.ap.rearrange("p (a b) -> p a b", b=128), pv.bufs),
                     [(V(h[:, kc, a * 128:(a + 1) * 128], b_h[tt]), V(w.ap[:, :, kc, :], w.bufs)) for kc in range(8)])
                yield
                k.copy(k.ACT, V(vv[:, CTX // 128 + a, :], b_v), pv)
                if not sample:
                    k.copy(k.DVE, V(vst[:], b_vst), pv)
                    self.store(self.dr["st_dv"].ap()[a * 128:(a + 1) * 128, j * 256:(j + 1) * 256], vst[:], [b_vst], "st")
                yield
            chains = []
            ci = 0
            for which in range(2):
                for mp in range(2):
                    for tt in range(NT):
                        chains.append((qk_chain, (which, mp, tt, ci)))
                        ci += 1
            for a in range(T // 128):
                chains.append((v_chain, (a,)))
            self.drain(self.rr((f(*args) for f, args in chains), W))
            items = [(hh, s0, S, q0) for hh in range(2) for (s0, S) in seqs for q0 in range(s0, s0 + S, QT)]

            def combine(n):
                hh, s0, S, q0 = items[n]
                o1, o2 = o12[n % 2]
                bo = b_o[n % 2]
                k.stt(k.DVE, V(o1[:, 0:QT], bo), V(o2[:, 0:QT], bo), V(self.lam[:, 1:2], self.b_lam), V(o1[:, 0:QT], bo), ALU.mult, ALU.add)
                yield
                yield from self.rms_g([V(o1[:, 0:QT], bo)], 1, 128, 128, lambda c: self.vcol("diff_hn"),
                                      [V(oT[:, hh, q0:q0 + QT], b_oT)], scrs[0]["sq"], extra_scale=(1.0 - lam_init), ring=[5, 6, 7])
            pending = None
            for n, (hh, s0, S, q0) in enumerate(items):
                pb = hh * 64
                kcols = list(range(0, CTX, 128)) + [CTX + s0 + a for a in range(0, S, 128)]
                for mp in range(2):
                    def sp(kc, kcols=kcols, mp=mp, hh=hh, q0=q0):
                        c0 = kcols[kc]
                        return [(V(kT[:, mp, c0:c0 + 128], b_k[mp]), V(qP[:, hh, mp, q0:q0 + QT], b_q[mp]))]
                    pending = self.attend(len(kcols), QT, sp,
                                          lambda kc, kcols=kcols, hh=hh: V(vv[:, kcols[kc] // 128, hh * 128:(hh + 1) * 128], b_v),
                                          pts, b_pts, V(o12[n % 2][mp][:, 0:QT], b_o[n % 2]), (rl, b_rl), side=pending, spi=1)
                self.drain(pending)
                pending = combine(n)
            self.drain(pending)
            self.out_proj(i, P, self.dr["diff_w_o"], oT, b_oT, 2, k0=2 * j)

    def gelu_tanh(self, out, x, scr, b):
        k = self.k
        k.act(scr, x, AF.Square)
        k.ts(k.DVE, scr, scr, 0.044715, ALU.mult, 1.0, ALU.add)
        k.tt(k.DVE, scr, scr, x, ALU.mult)
        k.act(scr, scr, AF.Sigmoid, scale=1.5957691216057308)
        k.tt(k.DVE, out, scr, x, ALU.mult)

    def gmlp(self, i, P):
        k, nc = self.k, self.nc
        T, NT = P["T"], P["NT"]
        ht = k.alloc("gm_h", [128, 8, 512], BF16); b_ht = Buf("ght")
        gated = k.alloc("gm_g", [128, 8, T], BF16); b_g = Buf("gmg")
        scr = self.nm_scratch()
        wv = k.alloc("gm_wv", [128, 8, 8, 128], BF16); b_wv = Buf("gmwv")
        k.dma(k.POOL, wv[:], self.dr["gmlp_w_in"].ap()[8:16].rearrange("n p k j -> p n k j"), None, out_bufs=[b_wv])
        wsT = k.alloc("gm_ws", [128, 8, 128], BF16); vnbc = k.alloc("gm_vn", [128, 1024], F32); bsbc = k.alloc("gm_bs", [128, 1024], F32)
        b_c = Buf("gmc")
        k.dma(k.POOL, wsT[:], self.dr["gmlp_wsT"].ap(), None, out_bufs=[b_c])
        k.dma(k.SP, vnbc[:], self.dr["gmlp_vn_bc"].ap(), None, out_bufs=[b_c])
        k.dma(k.SP, bsbc[:], self.dr["gmlp_bs_bc"].ap(), None, out_bufs=[b_c])
        vt = k.alloc("gm_vt", [128, 1024], F32); b_vt = Buf("gmvt")
        s1 = k.alloc("gm_s1", [128, 1024], F32); b_s1 = Buf("gms1")
        vnb = k.alloc("gm_vnb", [128, 1024], BF16); b_vnb = Buf("gmvnb")
        ssq = k.alloc("gm_ssq", [128, 2], F32); b_ssq = Buf("gmssq")
        mixed = k.alloc("gm_mixed", [128, 8, 512], F32); b_mx = Buf("gmmx")
        uf = k.alloc("gm_uf", [128, 512], F32); b_uf = Buf("gmuf")
        for tt in range(NT):
            self.normmod(i, 0, P, tt, V(ht[:], b_ht), scr)
            for s in range(4):
                a = tt * 4 + s
                pvs = []
                for nb in range(2):
                    pv = self.psum(128, 512, ring=[0, 1, 2, 3])
                    k.mm(V(pv.ap.rearrange("p (a b) -> p a b", b=128), pv.bufs),
                         [(V(ht[:, kc, s * 128:(s + 1) * 128], b_ht), V(wv[:, nb * 4:(nb + 1) * 4, kc, :], b_wv)) for kc in range(8)])
                    pvs.append(pv)
                    self.gelu_tanh(V(vt[:, nb * 512:(nb + 1) * 512], b_vt), pv, V(s1[:, nb * 512:(nb + 1) * 512], b_s1), b_s1)
                k.memset(k.DVE, V(ssq[:, 0:1], b_ssq), 0.0)
                k.op(k.ACT, [V(s1[:], b_s1), V(ssq[:, 0:1], b_ssq)], [V(vt[:], b_vt)],
                     lambda: nc.scalar.activation(out=s1[:], in_=vt[:], func=AF.Square, accum_out=ssq[:, 0:1]))
                k.act(V(ssq[:, 1:2], b_ssq), V(ssq[:, 0:1], b_ssq), AF.Ln, bias=self.cst(EPS), scale=1.0 / 1024)
                k.act(V(ssq[:, 1:2], b_ssq), V(ssq[:, 1:2], b_ssq), AF.Exp, scale=-0.5)
                k.stt(k.DVE, V(vnb[:], b_vnb), V(vt[:], b_vt), V(ssq[:, 1:2], b_ssq), V(vnbc[:], b_c), ALU.mult, ALU.mult)
                for nb in range(2):
                    pv = self.psum(128, 512, ring=[4, 5])
                    for gg in range(4):
                        g = nb * 4 + gg
                        k.mm(V(pv.ap[:, gg * 128:(gg + 1) * 128], pv.bufs), [(V(vnb[:, g * 128:(g + 1) * 128], b_vnb), V(wsT[:, g, :], b_c))])
                    k.tt(k.DVE, V(mixed[:, nb * 4:(nb + 1) * 4, s * 128:(s + 1) * 128], b_mx),
                         V(pv.ap.rearrange("p (a b) -> p a b", b=128), pv.bufs),
                         V(bsbc[:, nb * 512:(nb + 1) * 512].rearrange("p (a b) -> p a b", b=128), b_c), ALU.add)
            sl = slice(tt * 512, (tt + 1) * 512)
            for g in range(8):
                w = self.wtile(self.dr["gmlp_w_in"], g)
                pv = self.psum(128, 512, ring=[6, 7])
                k.mm(pv, [(V(w.ap[:, 0, kc, :], w.bufs), V(ht[:, kc, :], b_ht)) for kc in range(8)])
                self.gelu_tanh(V(uf[:], b_uf), pv, V(s1[:, 0:512], b_s1), b_s1)
                k.tt(k.DVE, V(gated[:, g, sl], b_g), V(uf[:], b_uf), V(mixed[:, g, :], b_mx), ALU.mult)
        self.out_proj(i, P, self.dr["gmlp_w_out"], gated, b_g, 8)


_CACHE = {}


def _get_prog(dbg=None):
    key = ("prog", dbg)
    if key not in _CACHE:
        _CACHE[key] = Prog(dbg)
    return _CACHE[key]


def _prep_shared(I):
    f = lambda a: np.ascontiguousarray(np.asarray(a, np.float32))
    sh = {}
    sh["ada_w"] = np.stack([_relayout_w(f(I["ada_w"][i])) for i in range(DEPTH)])
    sh["ffn_w_in"] = np.stack([_relayout_w(f(I["ffn_w_in"][i])) for i in range(DEPTH)])
    sh["ffn_w_out"] = np.stack([_relayout_w(f(I["ffn_w_out"][i])) for i in range(DEPTH)])
    wd = f(I["mla_w_down"][0])
    wd = np.concatenate([wd, np.zeros((D, 64), np.float32)], axis=1)
    sh["mla_w_down"] = _relayout_w(wd)
    uq = f(I["mla_w_uq"][0])
    sh["mla_w_uq"] = np.ascontiguousarray(uq.reshape(6, 128, 8, 192).transpose(2, 1, 0, 3))
    sh["mla_w_uk"] = _relayout_w(f(I["mla_w_uk"][0]))
    sh["mla_w_uv"] = _relayout_w(f(I["mla_w_uv"][0]))
    sh["mla_w_o"] = _relayout_w(f(I["mla_w_o"][0]))
    sh["diff_w_qkv"] = _relayout_w(f(I["diff_w_qkv"][0]))
    sh["diff_w_o"] = _relayout_w(f(I["diff_w_o"][0]))
    sh["sconv_w_in"] = _relayout_w(f(I["sconv_w_in"][0]))
    sh["sconv_w_out"] = _relayout_w(f(I["sconv_w_out"][0]))
    sh["gmlp_w_in"] = _relayout_w(f(I["gmlp_w_in"][0]))
    sh["gmlp_w_out"] = _relayout_w(f(I["gmlp_w_out"][0]))
    sh["gmlp_wsT"] = np.ascontiguousarray(f(I["gmlp_w_s"][0]).transpose(2, 0, 1))
    sh["gmlp_vn_bc"] = np.ascontiguousarray(np.broadcast_to(f(I["gmlp_v_norm"][0])[None, :], (128, 1024)))
    sh["gmlp_bs_bc"] = np.ascontiguousarray(np.broadcast_to(f(I["gmlp_b_s"][0]).reshape(1, 1024), (128, 1024)))
    vp = VecPack()
    for i in range(DEPTH):
        vp.add(f"n1g{i}", I["norm1_g"][i]); vp.add(f"n2g{i}", I["norm2_g"][i])
        for kk in range(3):
            vp.add(f"fcw{i}_{kk}", I["ffn_conv_w"][i][kk])
        vp.add(f"fcb{i}", I["ffn_conv_b"][i])
        vp.add(f"adab{i}", I["ada_b"][i])
    vp.add("mla_qnorm", I["mla_q_norm"][0]); vp.add("mla_kvnorm", I["mla_kv_norm"][0])
    vp.add("mla_qn_nope", I["mla_qn_nope"][0]); vp.add("mla_qn_rope", I["mla_qn_rope"][0])
    vp.add("mla_kn_nope", I["mla_kn_nope"][0]); vp.add("mla_kn_rope", I["mla_kn_rope"][0])
    vp.add("diff_qn", I["diff_qn"][0]); vp.add("diff_kn", I["diff_kn"][0]); vp.add("diff_hn", I["diff_head_norm"][0])
    for nm in ("lq1", "lk1", "lq2", "lk2"):
        vp.add("diff_" + nm, I["diff_" + nm][0])
    for kk in range(3):
        vp.add(f"scw{kk}", I["sconv_w"][0][kk])
    assert vp.idx == VIDX.idx
    sh["vecs"] = vp.pack()
    C, S = _rope_tables(2048)
    sh["ropeC"], sh["ropeS"] = C, S
    return sh


def _consts_host():
    ident = np.eye(128, dtype=np.float32)
    perm = np.zeros((128, 128), np.float32)
    for m in range(128):
        perm[m + 32 if m % 64 < 32 else m - 32, m] = 1.0
    o64 = np.zeros((128, 128), np.float32)
    o64[0:64, 0:64] = 1.0
    o64[64:, 64:] = 1.0
    return ident, perm, o64


def kernel(**I):
    dbg = I.pop("_dbg", None)
    ncores = I.pop("_ncores", NCORES)
    prog = _get_prog(dbg)
    f = lambda a: np.ascontiguousarray(np.asarray(a, np.float32))
    sh = _prep_shared(I)
    ident, perm, o64 = _consts_host()
    sh["c_ident"], sh["c_perm"], sh["c_ones64"] = ident, perm, o64
    in_maps = []
    for c in range(ncores):
        m = dict(sh)
        m["xT_p"] = np.ascontiguousarray(f(I["x_prompt"][2 * c:2 * c + 2]).reshape(512, D).T)
        m["xT_s"] = np.ascontiguousarray(f(I["x_sample"][c]).T)
        cond = np.stack([f(I["c_ctx"]), f(I["c"][c])], axis=-1)
        m["condT"] = np.ascontiguousarray(cond.reshape(8, 128, 2).transpose(1, 0, 2))
        m["ctx_ckvT"] = np.ascontiguousarray(f(I["cache_mla_ckv"][c, 0]).T)
        m["ctx_krT"] = np.ascontiguousarray(f(I["cache_mla_krope"][c, 0]).T)
        m["ctx_dkT"] = np.ascontiguousarray(f(I["cache_diff_k"][c, 0]).reshape(256, 1024).T)
        m["ctx_dv"] = np.ascontiguousarray(f(I["cache_diff_v"][c, 0]).reshape(256, 1024))
        in_maps.append(m)
    res = run_bass_kernel_spmd(prog.nc, in_maps, core_ids=list(range(ncores)))
    R = res.results
    NC_ = ncores
    yp = np.concatenate([R[c]["yT_p"].T.reshape(2, 256, D) for c in range(NC_)], axis=0)
    ys = np.stack([R[c]["yT_s"].T for c in range(NC_)], axis=0)
    ckv = np.concatenate([R[c]["st_ckvT"].T.reshape(2, 1, 256, 256) for c in range(NC_)], axis=0)
    kr = np.concatenate([R[c]["st_krT"].T.reshape(2, 1, 256, 64) for c in range(NC_)], axis=0)
    dk = np.concatenate([R[c]["st_dkT"].T.reshape(2, 1, 256, 2, 8, 64) for c in range(NC_)], axis=0)
    dv = np.concatenate([R[c]["st_dv"].reshape(2, 1, 256, 8, 128) for c in range(NC_)], axis=0)
    out = tuple(np.ascontiguousarray(a.astype(np.float32)) for a in (yp, ys, ckv, kr, dk, dv))
    return out
```

```python
import math
import numpy as np
import concourse.bass as bass
import concourse.mybir as mybir
from concourse.bass_utils import run_bass_kernel_spmd

F32 = mybir.dt.float32
BF16 = mybir.dt.bfloat16
AF = mybir.ActivationFunctionType
ALU = mybir.AluOpType

D = 1024
DEPTH = 4
EPS = 1e-6
FH = 2816
NCORES = 8
SB_START = 16640
SB_END = 229344


class Buf:
    __slots__ = ("name", "w", "r", "_ds")

    def __init__(self, name):
        self.name = name
        self.w = None
        self.r = {}


class V:
    __slots__ = ("ap", "bufs")

    def __init__(self, ap, bufs):
        self.ap = ap
        self.bufs = bufs if isinstance(bufs, (list, tuple)) else [bufs]


class Sem:
    def __init__(self, nc, name, dma=False):
        self.h = nc.alloc_semaphore(name)
        self.n = 0
        self.dma = dma
        self.waited = False


class Eng:
    def __init__(self, nc, e, name, selfsync=True):
        self.e = e
        self.sem = Sem(nc, "p_" + name)
        self.seen = {}
        self.selfsync = selfsync
        self.name = name


class K:
    def __init__(self, nc):
        self.nc = nc
        self.PE = Eng(nc, nc.tensor, "pe", selfsync=False)
        self.ACT = Eng(nc, nc.scalar, "act")
        self.DVE = Eng(nc, nc.vector, "dve")
        self.POOL = Eng(nc, nc.gpsimd, "pool")
        self.SP = Eng(nc, nc.sync, "sp")
        self.engs = [self.PE, self.ACT, self.DVE, self.POOL, self.SP]
        self.sb_off = SB_START
        self.nalloc = 0
        self.dsems = {}

    def alloc(self, name, shape, dtype):
        nbytes = int(np.prod(shape[1:])) * (2 if dtype == BF16 else 4)
        off = (self.sb_off + 63) // 64 * 64
        assert off + nbytes <= SB_END, f"SBUF overflow allocating {name}: {off}+{nbytes}"
        self.sb_off = off + nbytes
        self.nalloc += 1
        return self.nc.alloc_sbuf_tensor_at(f"{name}_{self.nalloc}", list(shape), dtype, offset=off)

    def mark(self):
        return self.sb_off

    def release(self, m):
        self.sb_off = m

    def dsem(self, name):
        if name not in self.dsems:
            self.dsems[name] = Sem(self.nc, "d_" + name, dma=True)
        return self.dsems[name]

    def _waits(self, eng, reads, writes):
        need = {}

        def add(tok):
            if tok is None:
                return
            s, v = tok
            if need.get(s, 0) < v:
                need[s] = v

        for b in reads:
            add(b.w)
        for b in writes:
            add(b.w)
            for s, v in b.r.items():
                add((s, v))
        for s, v in need.items():
            if s is eng.sem and not eng.selfsync:
                continue
            if s.dma:
                v = s.n
                s.waited = True
            if eng.seen.get(s, 0) >= v:
                continue
            eng.e.wait_ge(s.h, v)
            eng.seen[s] = v

    def _record(self, tok, reads, writes):
        s, v = tok
        for b in reads:
            if b.r.get(s, 0) < v:
                b.r[s] = v
        for b in writes:
            b.w = tok
            b.r = {}

    def op(self, eng, outs, ins, fn):
        reads = [b for v in ins for b in v.bufs]
        writes = [b for v in outs for b in v.bufs]
        self._waits(eng, reads, writes)
        inst = fn()
        eng.sem.n += 1
        inst.then_inc(eng.sem.h, 1)
        self._record((eng.sem, eng.sem.n), reads, writes)

    def dma(self, eng, out, in_, sem, out_bufs=(), in_bufs=()):
        b0 = (list(out_bufs) + list(in_bufs))[0]
        if hasattr(b0, "_ds"):
            sem = b0._ds
        else:
            self._pool_i = getattr(self, "_pool_i", 0) + 1
            sem = self.dsem(f"pool_{eng.name}_{self._pool_i % 8}")
        if sem.waited and eng.seen.get(sem, 0) < sem.n:
            eng.e.wait_ge(sem.h, sem.n)
            eng.seen[sem] = sem.n
        sem.waited = False
        self._waits(eng, list(in_bufs), list(out_bufs))
        inst = eng.e.dma_start(out=out, in_=in_)
        sem.n += 16
        inst.then_inc(sem.h, 16)
        self._record((sem, sem.n), list(in_bufs), list(out_bufs))

    def barrier(self):
        for e in self.engs:
            for o in self.engs:
                if o is e or o.sem.n == 0:
                    continue
                if e.seen.get(o.sem, 0) < o.sem.n:
                    e.e.wait_ge(o.sem.h, o.sem.n)
                    e.seen[o.sem] = o.sem.n
            for s in self.dsems.values():
                if s.n and e.seen.get(s, 0) < s.n:
                    e.e.wait_ge(s.h, s.n)
                    e.seen[s] = s.n
                    s.waited = True

    def mm(self, ps, pairs, eng=None):
        nc = self.nc
        ins = [x for p in pairs for x in p]

        def fn():
            inst = None
            n = len(pairs)
            for i, (l, r) in enumerate(pairs):
                inst = nc.tensor.matmul(ps.ap, lhsT=l.ap, rhs=r.ap, start=(i == 0), stop=(i == n - 1))
            return inst
        self.op(self.PE, [ps], ins, fn)

    def act(self, out, in_, func, bias=None, scale=None, eng=None):
        nc = self.nc
        ins = [in_]
        kw = {}
        if bias is not None:
            if isinstance(bias, V):
                ins.append(bias)
                kw["bias"] = bias.ap
            else:
                kw["bias"] = bias
        if scale is not None:
            if isinstance(scale, V):
                ins.append(scale)
                kw["scale"] = scale.ap
            else:
                kw["scale"] = scale
        self.op(self.ACT, [out], ins, lambda: nc.scalar.activation(out=out.ap, in_=in_.ap, func=func, **kw))

    def _sc(self, x, ins):
        if isinstance(x, V):
            ins.append(x)
            return x.ap
        return x

    def ts(self, eng, out, in0, s1, op0, s2=None, op1=None):
        ins = [in0]
        a1 = self._sc(s1, ins)
        a2 = self._sc(s2, ins)
        kw = {} if op1 is None else {"op1": op1}
        self.op(eng, [out], ins, lambda: eng.e.tensor_scalar(out=out.ap, in0=in0.ap, scalar1=a1, scalar2=a2, op0=op0, **kw))

    def stt(self, eng, out, in0, s, in1, op0, op1):
        ins = [in0, in1]
        a = self._sc(s, ins)
        self.op(eng, [out], ins, lambda: eng.e.scalar_tensor_tensor(out=out.ap, in0=in0.ap, scalar=a, in1=in1.ap, op0=op0, op1=op1))

    def tt(self, eng, out, in0, in1, op):
        self.op(eng, [out], [in0, in1], lambda: eng.e.tensor_tensor(out=out.ap, in0=in0.ap, in1=in1.ap, op=op))

    def copy(self, eng, out, in_):
        if eng is self.ACT:
            self.act(out, in_, AF.Identity)
        else:
            self.op(eng, [out], [in_], lambda: eng.e.tensor_copy(out=out.ap, in_=in_.ap))

    def recip(self, out, in_):
        nc = self.nc
        self.op(self.DVE, [out], [in_], lambda: nc.vector.reciprocal(out=out.ap, in_=in_.ap))

    def memset(self, eng, out, val):
        self.op(eng, [out], [], lambda: eng.e.memset(out.ap, val))


def _relayout_w(w):
    Kd, N = w.shape
    assert Kd % 128 == 0 and N % 128 == 0
    return np.ascontiguousarray(w.reshape(Kd // 128, 128, N // 128, 128).transpose(2, 1, 0, 3))


def _cols(v):
    v = np.asarray(v, np.float32).reshape(-1)
    if v.size < 128:
        v = np.tile(v, 128 // v.size)
    return np.ascontiguousarray(v.reshape(-1, 128).T)


class VecPack:
    def __init__(self):
        self.cols = []
        self.idx = {}
        self.n = 0

    def add(self, name, v):
        c = _cols(v)
        self.idx[name] = (self.n, c.shape[1])
        self.cols.append(c)
        self.n += c.shape[1]

    def pack(self):
        return np.ascontiguousarray(np.concatenate(self.cols, axis=1))


def _vec_index():
    vp = VecPack()
    z = np.zeros
    for i in range(DEPTH):
        vp.add(f"n1g{i}", z(D)); vp.add(f"n2g{i}", z(D))
        for k in range(3):
            vp.add(f"fcw{i}_{k}", z(2 * FH))
        vp.add(f"fcb{i}", z(2 * FH))
        vp.add(f"adab{i}", z(6 * D))
    vp.add("mla_qnorm", z(768)); vp.add("mla_kvnorm", z(256))
    vp.add("mla_qn_nope", z(128)); vp.add("mla_qn_rope", z(64))
    vp.add("mla_kn_nope", z(128)); vp.add("mla_kn_rope", z(64))
    vp.add("diff_qn", z(64)); vp.add("diff_kn", z(64)); vp.add("diff_hn", z(128))
    for nm in ("lq1", "lk1", "lq2", "lk2"):
        vp.add("diff_" + nm, z(64))
    for k in range(3):
        vp.add(f"scw{k}", z(D))
    return vp


VIDX = _vec_index()


def _rope_tables(S):
    rows = S // 64
    row = np.repeat(np.arange(rows, dtype=np.float32), 64)
    col = np.tile(np.arange(64, dtype=np.float32), rows)
    n_freq = 16
    inv_freq = (np.float32(10000.0) ** (-np.arange(n_freq, dtype=np.float32) / np.float32(n_freq))).astype(np.float32)
    ang = np.concatenate([row[:, None] * inv_freq, col[:, None] * inv_freq], axis=-1).astype(np.float32)
    cos = np.cos(ang).astype(np.float32).T
    sin = np.sin(ang).astype(np.float32).T
    C = np.concatenate([cos, cos, cos, cos], axis=0)
    Sg = np.concatenate([-sin, sin, -sin, sin], axis=0)
    return np.ascontiguousarray(C), np.ascontiguousarray(Sg)


class Prog:
    def __init__(self, dbg=None):
        self.dbg = dbg
        nc = bass.Bass("TRN2", target_bir_lowering=False)
        self.nc = nc
        self.k = K(nc)
        self.dr = {}
        self.build()

    def din(self, name, shape, dtype=F32):
        t = self.nc.dram_tensor(name, list(shape), dtype, kind="ExternalInput")
        self.dr[name] = t
        return t

    def dout(self, name, shape):
        t = self.nc.dram_tensor(name, list(shape), F32, kind="ExternalOutput")
        self.dr[name] = t
        return t

    def build(self):
        nc, k = self.nc, self.k
        self.din("xT_p", [D, 512]); self.din("xT_s", [D, 2048])
        self.din("condT", [128, 8, 2])
        self.din("vecs", [128, VIDX.n])
        self.din("ropeC", [128, 2048]); self.din("ropeS", [128, 2048])
        self.din("ada_w", [DEPTH, 48, 128, 8, 128])
        self.din("ffn_w_in", [DEPTH, 44, 128, 8, 128]); self.din("ffn_w_out", [DEPTH, 8, 128, 22, 128])
        self.din("mla_w_down", [9, 128, 8, 128]); self.din("mla_w_uq", [8, 128, 6, 192])
        self.din("mla_w_uk", [8, 128, 2, 128]); self.din("mla_w_uv", [8, 128, 2, 128]); self.din("mla_w_o", [8, 128, 8, 128])
        self.din("diff_w_qkv", [24, 128, 8, 128]); self.din("diff_w_o", [8, 128, 8, 128])
        self.din("sconv_w_in", [24, 128, 8, 128]); self.din("sconv_w_out", [8, 128, 8, 128])
        self.din("gmlp_w_in", [16, 128, 8, 128]); self.din("gmlp_w_out", [8, 128, 8, 128])
        self.din("gmlp_wsT", [128, 8, 128]); self.din("gmlp_vn_bc", [128, 1024]); self.din("gmlp_bs_bc", [128, 1024])
        self.din("ctx_ckvT", [256, 256]); self.din("ctx_krT", [64, 256])
        self.din("ctx_dkT", [1024, 256]); self.din("ctx_dv", [256, 1024])
        self.din("c_ident", [128, 128]); self.din("c_perm", [128, 128]); self.din("c_ones64", [128, 128])
        self.dout("yT_p", [D, 512]); self.dout("yT_s", [D, 2048])
        self.dout("st_ckvT", [256, 512]); self.dout("st_krT", [64, 512])
        self.dout("st_dkT", [1024, 512]); self.dout("st_dv", [512, 1024])

        self.vecs = k.alloc("vecs", [128, VIDX.n], F32); self.b_vecs = Buf("vecs")
        self.ident = k.alloc("ident", [128, 128], F32)
        self.perm = k.alloc("perm", [128, 128], F32)
        self.ones_bf = k.alloc("ones_bf", [128, 128], BF16)
        self.perm_bf = k.alloc("perm_bf", [128, 128], BF16)
        self.ones64 = k.alloc("ones64", [128, 128], BF16)
        self.ones_f = k.alloc("ones_f", [128, 128], F32)
        self.b_const = Buf("const")
        self.mod = k.alloc("mod", [128, DEPTH, 48, 2], F32); self.b_mods = [Buf(f"mod{i}") for i in range(DEPTH)]
        self.gs = k.alloc("gs", [128, DEPTH, 2, 2, 8], F32)
        self.lam = k.alloc("lam", [128, 4], F32); self.b_lam = Buf("lam")
        self._eps = k.alloc("epscol", [128, 1], F32)
        k.memset(k.DVE, V(self._eps[:], self.b_const), EPS)
        self._cst = {}
        lam1 = 0.8 - 0.6 * math.exp(-0.3 * 1)
        for val in (EPS, 0.0, math.log(192.0 ** -0.5), math.log(64.0 ** -0.5), math.log(1.0 - lam1)):
            t = k.alloc("cst", [128, 1], F32)
            k.memset(k.DVE, V(t[:], self.b_const), float(val))
            self._cst[round(float(val), 9)] = t
        self.y = k.alloc("y", [128, 8, 2048], F32)
        self.b_y = [Buf(f"y{t}") for t in range(4)]
        self.wslots = [k.alloc(f"wslot{i}", [128, 2048], BF16) for i in range(4)]
        self.b_w = [Buf(f"w{i}") for i in range(4)]
        self.wi = 0
        self.ps = [nc.alloc_psum_tensor(f"ps{i}", [128, 512], F32) for i in range(8)]
        self.b_ps = [Buf(f"ps{i}") for i in range(8)]
        self.psi = 0

        for b in [self.b_vecs, self.b_const] + self.b_y + self.b_w:
            b._ds = k.dsem("own_" + b.name)
        self.consts()
        self.preamble()
        for pi, P in enumerate(self.passes()):
            if self.dbg is not None and self.dbg[1] == "p" and P["sample"]:
                continue
            self.run_pass(P)
        self.finish()

    def vcol(self, name, c=0, rows=128):
        o, n = VIDX.idx[name]
        assert c < n
        return V(self.vecs[0:rows, o + c:o + c + 1], self.b_vecs)

    def psum(self, rows=128, cols=512, ring=None):
        ring = ring or list(range(8))
        key = tuple(ring)
        if not hasattr(self, "_rings"):
            self._rings = {}
        i = self._rings.get(key, 0)
        self._rings[key] = i + 1
        b = ring[i % len(ring)]
        return V(self.ps[b][0:rows, 0:cols], self.b_ps[b])

    def wload(self, parts, total_cols):
        k = self.k
        i = self.wi % 4
        self.wi += 1
        slot, b = self.wslots[i], self.b_w[i]
        assert total_cols <= 2048
        sem = None
        for (o, n, src, shp) in parts:
            dst = slot[:, o:o + n]
            if len(shp) == 2:
                dst = dst.rearrange("p (a b) -> p a b", a=shp[0])
            elif len(shp) == 3:
                dst = dst.rearrange("p (a b c) -> p a b c", a=shp[0], b=shp[1])
            k.dma(k.POOL, dst, src, sem, out_bufs=[b])
        return slot, b

    def wtile(self, dram, n0, nn=1):
        KC, W = dram.shape[2], dram.shape[3]
        src = dram.ap()[n0:n0 + nn].rearrange("n p k j -> p n k j")
        slot, b = self.wload([(0, nn * KC * W, src, (nn, KC, W))], nn * KC * W)
        return V(slot[:, 0:nn * KC * W].rearrange("p (n k j) -> p n k j", n=nn, k=KC), b)

    def consts(self):
        nc, k = self.nc, self.k
        s = None
        k.dma(k.SP, self.vecs[:], self.dr["vecs"].ap(), s, out_bufs=[self.b_vecs])
        bc = [self.b_const]
        k.dma(k.SP, self.ident[:], self.dr["c_ident"].ap(), s, out_bufs=bc)
        k.dma(k.SP, self.perm[:], self.dr["c_perm"].ap(), s, out_bufs=bc)
        k.op(k.DVE, [V(None, bc)], [V(None, bc)], lambda: nc.vector.tensor_copy(out=self.perm_bf[:], in_=self.perm[:]))
        k.op(k.POOL, [V(None, bc)], [], lambda: nc.gpsimd.memset(self.ones64[:], 0.0))
        k.op(k.POOL, [V(None, bc)], [], lambda: nc.gpsimd.memset(self.ones64[0:64, 0:64], 1.0))
        k.op(k.POOL, [V(None, bc)], [], lambda: nc.gpsimd.memset(self.ones64[64:128, 64:128], 1.0))
        k.op(k.POOL, [V(None, bc)], [], lambda: nc.gpsimd.memset(self.ones_bf[:], 1.0))
        k.op(k.POOL, [V(None, bc)], [], lambda: nc.gpsimd.memset(self.ones_f[:], 1.0))

    def cv(self, ap):
        return V(ap, self.b_const)

    def ada_gen(self, i):
        k = self.k
        aw = self.dr["ada_w"].ap()
        o, n = VIDX.idx[f"adab{i}"]
        bm = self.b_mods[i]
        for g in range(6):
            a, ba = self.abuf[self.ada_it % 2], self.b_a[self.ada_it % 2]
            self.ada_it += 1
            src = aw[i, g * 8:(g + 1) * 8].rearrange("n p k j -> p n k j")
            k.dma(k.POOL, a[:], src, None, out_bufs=[ba])
            pv = self.psum(128, 16)
            for n_ in range(8):
                k.mm(V(pv.ap[:, 2 * n_:2 * n_ + 2], pv.bufs),
                     [(V(a[:, n_, kc, :], ba), V(self.cond_b[:, kc, :], self.b_cond)) for kc in range(8)])
            bias = self.vecs[:, o + g * 8:o + (g + 1) * 8]
            k.tt(k.DVE, V(self.mod[:, i, g * 8:(g + 1) * 8, :], bm),
                 V(pv.ap.rearrange("p (c t) -> p c t", t=2), pv.bufs),
                 V(bias.unsqueeze(2).to_broadcast([128, 8, 2]), self.b_vecs), ALU.add)
            yield
        for w, (gname, sc0) in enumerate(((f"n1g{i}", 8), (f"n2g{i}", 32))):
            go, _ = VIDX.idx[gname]
            for cd in range(2):
                k.stt(k.DVE, V(self.gs[:, i, w, cd, :], bm), V(self.mod[:, i, sc0:sc0 + 8, cd], bm), 1.0,
                      V(self.vecs[:, go:go + 8], self.b_vecs), ALU.add, ALU.mult)
        yield

    def preamble(self):
        nc, k = self.nc, self.k
        self.ada_m0 = k.mark()
        cond_f = k.alloc("cond_f", [128, 8, 2], F32)
        self.cond_b = k.alloc("cond_b", [128, 8, 2], BF16)
        self.b_cond = Buf("cond")
        b_c = self.b_cond
        self.abuf = [k.alloc(f"adaw{i}", [128, 8, 8, 128], BF16) for i in range(2)]
        self.b_a = [Buf(f"adaw{i}") for i in range(2)]
        self.ada_it = 0
        k.dma(k.SP, cond_f[:], self.dr["condT"].ap(), None, out_bufs=[b_c])
        k.act(V(self.cond_b[:], b_c), V(cond_f[:], b_c), AF.Silu)
        self.drain(self.ada_gen(0))
        pr = k.alloc("lam_pr", [128, 2], F32)
        b_l = Buf("lamtmp")
        for j, (a, b) in enumerate((("diff_lq1", "diff_lk1"), ("diff_lq2", "diff_lk2"))):
            k.tt(k.DVE, V(pr[0:64, j:j + 1], b_l), self.vcol(a, 0, 64), self.vcol(b, 0, 64), ALU.mult)
        pl = self.psum(128, 2)
        k.mm(pl, [(self.cv(self.ones_f[0:64, :]), V(pr[0:64, :], b_l))])
        ex = k.alloc("lam_ex", [128, 2], F32)
        k.act(V(ex[:], b_l), pl, AF.Exp)
        lam_init = 0.8 - 0.6 * math.exp(-0.3 * 1)
        k.stt(k.DVE, V(self.lam[:, 0:1], self.b_lam), V(ex[:, 0:1], b_l), lam_init, V(ex[:, 1:2], b_l), ALU.add, ALU.subtract)
        k.ts(k.DVE, V(self.lam[:, 1:2], self.b_lam), V(self.lam[:, 0:1], self.b_lam), -1.0, ALU.mult)
        k.barrier()

    def passes(self):
        return [
            dict(name="p", T=512, NT=1, seqs=[(0, 256), (256, 256)], cond=0, sample=False,
                 xin=self.dr["xT_p"], yout=self.dr["yT_p"], QT=256),
            dict(name="s", T=2048, NT=4, seqs=[(0, 2048)], cond=1, sample=True,
                 xin=self.dr["xT_s"], yout=self.dr["yT_s"], QT=512),
        ]

    def yv(self, tt, c=None):
        if c is None:
            return V(self.y[:, :, tt * 512:(tt + 1) * 512], self.b_y[tt])
        return V(self.y[:, c, tt * 512:(tt + 1) * 512], self.b_y[tt])

    def modcol(self, i, which, cond, c):
        return V(self.mod[:, i, which * 8 + c, cond:cond + 1], self.b_mods[i])

    def run_pass(self, P):
        nc, k = self.nc, self.k
        T, NT = P["T"], P["NT"]
        xin = P["xin"].ap().rearrange("(c p) t -> p c t", p=128)
        for tt in range(NT):
            k.dma(k.SP, self.y[:, :, tt * 512:(tt + 1) * 512], xin[:, :, tt * 512:(tt + 1) * 512], None,
                  out_bufs=[self.b_y[tt]])
        stop = self.dbg
        for i in range(DEPTH):
            if stop is not None and stop[0] < 0:
                break
            m0 = k.mark()
            [self.mla, self.diff, self.sconv, self.gmlp][i % 4](i, P)
            k.barrier()
            k.release(m0)
            if stop == (i, "m"):
                break
            m0 = k.mark()
            side = None
            if not P["sample"] and i + 1 < DEPTH:
                side = self.ada_gen(i + 1)
            self.ffn(i, P, side=side)
            self.drain(side)
            k.barrier()
            k.release(m0)
            if stop == (i, "f"):
                break
        yout = P["yout"].ap().rearrange("(c p) t -> p c t", p=128)
        for tt in range(NT):
            k.dma(k.SP, yout[:, :, tt * 512:(tt + 1) * 512], self.y[:, :, tt * 512:(tt + 1) * 512], None,
                  in_bufs=[self.b_y[tt]])
        k.barrier()
        if not P["sample"]:
            k.release(self.ada_m0)

    def finish(self):
        k = self.k
        for s in k.dsems.values():
            if s.n and k.SP.seen.get(s, 0) < s.n:
                k.SP.e.wait_ge(s.h, s.n)
                k.SP.seen[s] = s.n

    def normmod_g(self, i, which, P, tt, h_out, scr):
        k = self.k
        cd = P["cond"]
        sq, rs, tmp, b_s = scr["sq"], scr["rs"], scr["tmp"], scr["b"]
        yv = self.yv(tt)
        k.act(V(sq[:], b_s), yv, AF.Square)
        yield
        pv = self.psum()
        k.mm(pv, [(self.cv(self.ones_bf[:]), V(sq[:, c, :], b_s)) for c in range(8)])
        yield
        k.act(V(rs[:], b_s), pv, AF.Ln, bias=self.cst(EPS), scale=1.0 / D)
        yield
        k.act(V(rs[:], b_s), V(rs[:], b_s), AF.Exp, scale=-0.5)
        yield
        for c in range(8):
            tv = V(tmp[:, c % 2, :], scr["bt"][c % 2])
            k.stt(k.DVE, tv, self.yv(tt, c), V(self.gs[:, i, which, cd, c:c + 1], self.b_mods[i]), V(rs[:], b_s), ALU.mult, ALU.mult)
            k.act(V(h_out.ap[:, c, :], h_out.bufs), tv, AF.Identity, bias=self.modcol(i, 3 * which, cd, c))
            if c % 2 == 1:
                yield

    def normmod(self, *a):
        self.drain(self.normmod_g(*a))

    def normmod_all(self, i, which, P, h, b_h):
        NT = P["NT"]
        scrs = [self.nm_scratch() for _ in range(min(2, NT))]
        self.drain(self.rr([self.normmod_g(i, which, P, tt, V(h[:, :, tt * 512:(tt + 1) * 512], b_h[tt]), scrs[tt % len(scrs)])
                            for tt in range(NT)], len(scrs)))

    def eps_col(self):
        return V(self._eps[:], self.b_const)

    def nm_scratch(self):
        k = self.k
        return dict(sq=k.alloc("nm_sq", [128, 8, 512], BF16), rs=k.alloc("nm_rs", [128, 512], F32),
                    tmp=k.alloc("nm_tmp", [128, 2, 512], F32), b=Buf("nm"), bt=[Buf("nmt0"), Buf("nmt1")])

    def resid(self, i, which, P, tt, m, pv, eng=None):
        k = self.k
        yv = self.yv(tt, m)
        k.stt(eng or k.DVE, yv, pv, self.modcol(i, 3 * which + 2, P["cond"], m), yv, ALU.mult, ALU.add)

    def out_proj(self, i, P, wdram, src, b_src, KC, k0=0):
        k = self.k
        for m in range(8):
            if KC == wdram.shape[2]:
                w = self.wtile(wdram, m)
                wap = w.ap[:, 0]
            else:
                srcw = wdram.ap()[m, :, k0:k0 + KC, :]
                slot, bw = self.wload([(0, KC * 128, srcw, (KC, 128))], KC * 128)
                w = V(slot[:, 0:KC * 128].rearrange("p (k j) -> p k j", k=KC), bw)
                wap = w.ap
            for tt in range(P["NT"]):
                pv = self.psum(ring=[5, 6, 7])
                k.mm(pv, [(V(wap[:, kc, :], w.bufs), V(src[:, kc, tt * 512:(tt + 1) * 512], b_src)) for kc in range(KC)])
                self.resid(i, 0, P, tt, m, pv)

    def ffn(self, i, P, side=None):
        k = self.k
        T, NT, seqs = P["T"], P["NT"], P["seqs"]
        Q = 2
        JJ = 22 // Q
        h = k.alloc("ffn_h", [128, 8, T], BF16); b_h = [Buf(f"fh{t}") for t in range(NT)]
        act = k.alloc("ffn_act", [128, JJ, T], BF16); b_act = Buf("fact")
        m1 = k.mark()
        self.normmod_all(i, 1, P, h, b_h)
        k.barrier()
        k.release(m1)
        npad = T + 2 * len(seqs)
        araw = [k.alloc(f"ffn_araw{g}", [128, npad], F32) for g in range(2)]
        b_ar = [Buf("arg"), Buf("aru")]
        cbuf = [k.alloc(f"ffn_c{g}", [128, T], F32) for g in range(2)]
        b_c = [Buf("cg"), Buf("cu")]
        for g in range(2):
            k.memset(k.POOL, V(araw[g][:], b_ar[g]), 0.0)
        win = self.dr["ffn_w_in"].ap()
        wout = self.dr["ffn_w_out"].ap()
        for q in range(Q):
            for jj in range(JJ):
                j = q * JJ + jj
                if side is not None and j % 3 == 1:
                    next(side, None)
                srcs = [win[i, j], win[i, 22 + j]]
                slot, bw = self.wload([(g * 1024, 1024, srcs[g], (8, 128)) for g in range(2)], 2048)
                wv = slot[:, 0:2048].rearrange("p (g k j) -> p g k j", g=2, k=8)
                for g in range(2):
                    for tt in range(NT):
                        pv = self.psum()
                        k.mm(pv, [(V(wv[:, g, kc, :], bw), V(h[:, kc, tt * 512:(tt + 1) * 512], b_h[tt])) for kc in range(8)])
                        for si, (s0, S) in enumerate(seqs):
                            lo, hi = max(s0, tt * 512), min(s0 + S, (tt + 1) * 512)
                            if lo >= hi:
                                continue
                            base = si * (S + 2) + 1 + (lo - s0)
                            k.copy(k.ACT, V(araw[g][:, base:base + hi - lo], b_ar[g]),
                                   V(pv.ap[:, lo - tt * 512:hi - tt * 512], pv.bufs))
                    col = j if g == 0 else 22 + j
                    for si, (s0, S) in enumerate(seqs):
                        base = si * (S + 2)
                        cv = V(cbuf[g][:, s0:s0 + S], b_c[g])
                        k.act(cv, V(araw[g][:, base + 1:base + 1 + S], b_ar[g]), AF.Identity,
                              bias=self.vcol(f"fcb{i}", col), scale=self.vcol(f"fcw{i}_1", col))
                        k.stt(k.DVE, cv, V(araw[g][:, base:base + S], b_ar[g]), self.vcol(f"fcw{i}_0", col), cv, ALU.mult, ALU.add)
                        k.stt(k.DVE, cv, V(araw[g][:, base + 2:base + 2 + S], b_ar[g]), self.vcol(f"fcw{i}_2", col), cv, ALU.mult, ALU.add)
                k.act(V(cbuf[0][:], b_c[0]), V(cbuf[0][:], b_c[0]), AF.Silu)
                k.tt(k.DVE, V(act[:, jj, :], b_act), V(cbuf[0][:], b_c[0]), V(cbuf[1][:], b_c[1]), ALU.mult)
            for m in range(8):
                src = wout[i, m, :, q * JJ:(q + 1) * JJ, :]
                slot, bw = self.wload([(0, JJ * 128, src, (JJ, 128))], JJ * 128)
                wv = slot[:, 0:JJ * 128].rearrange("p (k j) -> p k j", k=JJ)
                for tt in range(NT):
                    pv = self.psum()
                    k.mm(pv, [(V(wv[:, jj, :], bw), V(act[:, jj, tt * 512:(tt + 1) * 512], b_act)) for jj in range(JJ)])
                    self.resid(i, 1, P, tt, m, pv)

    def sconv(self, i, P):
        k = self.k
        T, NT, seqs = P["T"], P["NT"], P["seqs"]
        h = k.alloc("sc_h", [128, 8, T], BF16); b_h = [Buf(f"sh{t}") for t in range(NT)]
        g = k.alloc("sc_g", [128, 8, T], BF16); b_g = Buf("scg")
        npad = T + 2 * len(seqs)
        tpad = k.alloc("sc_t", [128, npad], F32); b_t = Buf("sct")
        gb = k.alloc("sc_gb", [128, T], F32); b_gb = Buf("scgb")
        gc = k.alloc("sc_gc", [128, 512], F32); b_gc = Buf("scgc")
        cv_t = k.alloc("sc_cv", [128, T], F32); b_cv = Buf("sccv")
        m1 = k.mark()
        self.normmod_all(i, 0, P, h, b_h)
        k.barrier()
        k.release(m1)
        k.memset(k.POOL, V(tpad[:], b_t), 0.0)
        for c in range(8):
            wts = [self.wtile(self.dr["sconv_w_in"], a * 8 + c) for a in range(3)]
            for tt in range(NT):
                sl = slice(tt * 512, (tt + 1) * 512)
                pvs = []
                for a in range(3):
                    pv = self.psum()
                    k.mm(pv, [(V(wts[a].ap[:, 0, kc, :], wts[a].bufs), V(h[:, kc, sl], b_h[tt])) for kc in range(8)])
                    pvs.append(pv)
                k.copy(k.ACT, V(gb[:, sl], b_gb), pvs[0])
                k.copy(k.ACT, V(gc[:], b_gc), pvs[1])
                for si, (s0, S) in enumerate(seqs):
                    lo, hi = max(s0, tt * 512), min(s0 + S, (tt + 1) * 512)
                    if lo >= hi:
                        continue
                    base = si * (S + 2) + 1 + (lo - s0)
                    k.tt(k.DVE, V(tpad[:, base:base + hi - lo], b_t), V(pvs[2].ap[:, lo - tt * 512:hi - tt * 512], pvs[2].bufs),
                         V(gc[:, lo - tt * 512:hi - tt * 512], b_gc), ALU.mult)
            for si, (s0, S) in enumerate(seqs):
                base = si * (S + 2)
                cv = V(cv_t[:, s0:s0 + S], b_cv)
                k.ts(k.DVE, cv, V(tpad[:, base + 1:base + 1 + S], b_t), self.vcol("scw1", c), ALU.mult)
                k.stt(k.DVE, cv, V(tpad[:, base:base + S], b_t), self.vcol("scw0", c), cv, ALU.mult, ALU.add)
                k.stt(k.DVE, cv, V(tpad[:, base + 2:base + 2 + S], b_t), self.vcol("scw2", c), cv, ALU.mult, ALU.add)
            k.tt(k.DVE, V(g[:, c, :], b_g), V(cv_t[:], b_cv), V(gb[:], b_gb), ALU.mult)
        self.out_proj(i, P, self.dr["sconv_w_out"], g, b_g, 8)


    S_RING = [0, 1]
    SIDE_RING = [2, 5, 6, 7]

    def cst(self, val, rows=128):
        key = round(float(val), 9)
        return V(self._cst[key][0:rows, :], self.b_const)

    @staticmethod
    def drain(g):
        if g is not None:
            for _ in g:
                pass

    @staticmethod
    def rr(chains, window):
        it = iter(chains)
        active = []
        while True:
            while len(active) < window:
                g = next(it, None)
                if g is None:
                    break
                active.append(g)
            if not active:
                return
            for g in list(active):
                try:
                    next(g)
                except StopIteration:
                    active.remove(g)
                yield

    def rms_g(self, src, nch, rows, dim, gain_fn, outs, scr, ones=None, extra_scale=1.0, ring=None):
        k = self.k
        sq, rs, b = scr
        N = src[0].ap.shape[-1]
        for c in range(nch):
            k.act(V(sq[0:rows, c, 0:N], b), src[c], AF.Square)
        yield
        pv = self.psum(rows, N, ring=ring or self.SIDE_RING)
        on = ones if ones is not None else self.ones_bf
        k.mm(pv, [(self.cv(on[0:rows, 0:rows]), V(sq[0:rows, c, 0:N], b)) for c in range(nch)])
        yield
        rsv = V(rs[0:rows, 0:N], b)
        k.act(rsv, pv, AF.Ln, bias=self.cst(EPS, rows), scale=1.0 / dim)
        yield
        k.act(rsv, rsv, AF.Exp, bias=self.cst(math.log(extra_scale), rows), scale=-0.5)
        yield
        for c in range(nch):
            k.stt(k.DVE, outs[c], src[c], gain_fn(c), rsv, ALU.mult, ALU.mult)
        yield

    def rms_fm(self, *a, **kw):
        self.drain(self.rms_g(*a, **kw))

    def rope_g(self, x, out, rows, t0, N, scr, ring=None):
        k = self.k
        t1, b = scr
        if not hasattr(self, "_xb"):
            self._xb = {}
        key = id(t1)
        xb = self._xb[key]
        k.copy(k.ACT, V(xb[0:rows, 0:N], b), x)
        yield
        pv = self.psum(rows, N, ring=ring or self.SIDE_RING)
        k.mm(pv, [(self.cv(self.perm_bf[0:rows, 0:rows]), V(xb[0:rows, 0:N], b))])
        yield
        k.tt(k.DVE, V(t1[0:rows, 0:N], b), pv, V(self.ropeS[0:rows, t0:t0 + N], self.b_rope), ALU.mult)
        k.tt(k.DVE, x, x, V(self.ropeC[0:rows, t0:t0 + N], self.b_rope), ALU.mult)
        yield
        if isinstance(out, list):
            for (r0, r1, ov) in out:
                k.tt(k.DVE, ov, V(x.ap[r0:r1], x.bufs), V(t1[r0:r1, 0:N], b), ALU.add)
        else:
            k.tt(k.DVE, out, x, V(t1[0:rows, 0:N], b), ALU.add)
        yield

    def rope_fm(self, *a, **kw):
        self.drain(self.rope_g(*a, **kw))

    def load_rope(self):
        k = self.k
        self.ropeC = k.alloc("ropeC", [128, 2048], F32)
        self.ropeS = k.alloc("ropeS", [128, 2048], F32)
        self.b_rope = Buf("rope")
        k.dma(k.SP, self.ropeC[:], self.dr["ropeC"].ap(), None, out_bufs=[self.b_rope])
        k.dma(k.SP, self.ropeS[:], self.dr["ropeS"].ap(), None, out_bufs=[self.b_rope])

    def attend(self, nk, QT, score_pairs, v_lhsT, pts, b_pts, out, rl_scr, side=None, spi=1):
        k = self.k
        po = V(self.ps[3][:, 0:QT], self.b_ps[3])
        pl = V(self.ps[4][:, 0:QT], self.b_ps[4])
        pss = [None] * nk
        pss[0] = self.psum(128, QT, ring=self.S_RING)
        k.mm(pss[0], score_pairs(0))
        for kc in range(nk):
            if kc + 1 < nk:
                pss[kc + 1] = self.psum(128, QT, ring=self.S_RING)
                k.mm(pss[kc + 1], score_pairs(kc + 1))
            pt = V(pts[kc % 3][:, 0:QT], b_pts[kc % 3])
            k.act(pt, pss[kc], AF.Exp)
            self.mm1(po, v_lhsT(kc), pt, kc == 0, kc == nk - 1)
            self.mm1(pl, self.cv(self.ones_bf[:]), pt, kc == 0, kc == nk - 1)
            if side is not None:
                for _ in range(spi):
                    if next(side, "done") == "done":
                        side = None
                        break
        rl, b_rl = rl_scr
        rlv = V(rl[:, 0:QT], b_rl)
        k.act(rlv, pl, AF.Ln)
        k.act(rlv, rlv, AF.Exp, scale=-1.0)
        k.tt(k.DVE, out, po, rlv, ALU.mult)
        return side

    def mm1(self, ps, l, r, start, stop):
        nc = self.nc
        self.k.op(self.k.PE, [ps], [l, r], lambda: nc.tensor.matmul(ps.ap, lhsT=l.ap, rhs=r.ap, start=start, stop=stop))

    def store(self, dram_ap, src_ap, bufs, name):
        k = self.k
        k.dma(k.SP, dram_ap, src_ap, None, in_bufs=bufs)

    def mla(self, i, P):
        k, nc = self.k, self.nc
        T, NT, seqs, sample, QT = P["T"], P["NT"], P["seqs"], P["sample"], P["QT"]
        CTX = 256 if sample else 0
        LT = T + CTX
        scale = 192.0 ** -0.5
        cqn = k.alloc("ml_cqn", [128, 6, T], BF16); b_cqn = Buf("cqn")
        ckvn = k.alloc("ml_ckvn", [128, 2, LT], BF16); b_ckvn = Buf("ckvn")
        krall = k.alloc("ml_kr", [128, LT], BF16); b_kr = Buf("krall")
        k.memset(k.POOL, V(krall[64:128, :], b_kr), 0.0)
        if sample:
            self.load_rope()
            k.dma(k.POOL, ckvn[:, :, 0:256], self.dr["ctx_ckvT"].ap().rearrange("(c p) t -> p c t", p=128), None, out_bufs=[b_ckvn])
            k.dma(k.POOL, krall[0:64, 0:256], self.dr["ctx_krT"].ap(), None, out_bufs=[b_kr])
        m1 = k.mark()
        scr = self.nm_scratch()
        ht = k.alloc("ml_ht", [128, 8, 512], BF16); b_ht = Buf("ht")
        dn = k.alloc("ml_dn", [128, 9, 512], F32); b_dn = Buf("dn")
        sqs = [(k.alloc(f"ml_sq{j}", [128, nchs, 512], BF16), k.alloc(f"ml_rs{j}", [128, 512], F32), Buf(f"mlsq{j}"))
               for j, nchs in enumerate((6, 2, 1))]
        t1 = k.alloc("ml_t1", [128, 512], F32); b_t1 = Buf("mlt1")
        self._xb = getattr(self, "_xb", {}); self._xb[id(t1)] = k.alloc("ml_xb", [128, 512], BF16)
        stf = k.alloc("ml_stf", [128, 3, 512], F32); b_stf = Buf("stf")
        wd = self.dr["mla_w_down"]
        for tt in range(NT):
            sl = slice(tt * 512, (tt + 1) * 512)
            self.normmod(i, 0, P, tt, V(ht[:], b_ht), scr)
            for n in range(9):
                rows = 128 if n < 8 else 64
                w = self.wtile(wd, n)
                pv = self.psum(rows, 512, ring=[5, 6, 7])
                k.mm(pv, [(V(w.ap[:, 0, kc, 0:rows], w.bufs), V(ht[:, kc, :], b_ht)) for kc in range(8)])
                k.copy(k.ACT if n % 2 else k.DVE, V(dn[0:rows, n, :], b_dn), pv)
            if sample:
                outs = [V(ckvn[:, c, CTX + tt * 512:CTX + (tt + 1) * 512], b_ckvn) for c in range(2)]
            else:
                outs = [V(stf[:, c, :], b_stf) for c in range(2)]
            krf = V(stf[0:64, 2, :], b_stf)

            def kr_chain(tt=tt, krf=krf):
                yield from self.rms_g([V(dn[0:64, 8, :], b_dn)], 1, 64, 64, lambda c: self.vcol("mla_kn_rope", 0, 64), [krf], sqs[2],
                                      ring=[0, 1, 2, 3, 4])
                if sample:
                    yield from self.rope_g(krf, V(krall[0:64, CTX + tt * 512:CTX + (tt + 1) * 512], b_kr), 64, tt * 512, 512, (t1, b_t1),
                                           ring=[0, 1, 2, 3, 4])
            chains = [
                self.rms_g([V(dn[:, c, :], b_dn) for c in range(6)], 6, 128, 768, lambda c: self.vcol("mla_qnorm", c),
                           [V(cqn[:, c, sl], b_cqn) for c in range(6)], sqs[0], ring=[0, 1, 2, 3, 4]),
                self.rms_g([V(dn[:, 6 + c, :], b_dn) for c in range(2)], 2, 128, 256, lambda c: self.vcol("mla_kvnorm", c),
                           outs, sqs[1], ring=[0, 1, 2, 3, 4]),
                kr_chain(),
            ]
            self.drain(self.rr(chains, 3))
            if not sample:
                for c in range(2):
                    k.copy(k.ACT, V(ckvn[:, c, sl], b_ckvn), V(stf[:, c, :], b_stf))
                k.copy(k.ACT, V(krall[0:64, sl], b_kr), krf)
                self.store(self.dr["st_ckvT"].ap().rearrange("(c p) t -> p c t", p=128)[:, :, sl], stf[:, 0:2, :], [b_stf], "st")
                self.store(self.dr["st_krT"].ap()[:, sl], stf[0:64, 2, :], [b_stf], "st")
        k.barrier()
        k.release(m1)
        LC = LT // 128
        oT = k.alloc("ml_oT", [128, 2, T], BF16); b_oT = Buf("oT")
        knh = [k.alloc(f"ml_knh{j}", [128, LT], BF16) for j in range(2)]; b_knh = [Buf(f"knh{j}") for j in range(2)]
        vh = [k.alloc(f"ml_vh{j}", [128, LC, 128], BF16) for j in range(2)]; b_vh = [Buf(f"vh{j}") for j in range(2)]
        wuqs = [k.alloc(f"ml_wuq{j}", [128, 6, 192], BF16) for j in range(2)]; b_wuq = [Buf(f"wuq{j}") for j in range(2)]
        qn = [k.alloc(f"ml_qn{j}", [128, 512], BF16) for j in range(2)]
        qr = [k.alloc(f"ml_qr{j}", [128, 512], BF16) for j in range(2)]
        qrf = [k.alloc(f"ml_qrf{j}", [64, 512], F32) for j in range(2)]
        b_q = [Buf(f"mlq{j}") for j in range(2)]
        for j in range(2):
            k.memset(k.POOL, V(qr[j][64:128, :], b_q[j]), 0.0)
        sqs = [(k.alloc(f"ml_sqb{j}", [128, 1, 512], BF16), k.alloc(f"ml_rsb{j}", [128, 512], F32), Buf(f"mlsqb{j}")) for j in range(3)]
        t1 = k.alloc("ml_t12", [128, 512], F32); b_t1 = Buf("mlt12")
        self._xb = getattr(self, "_xb", {}); self._xb[id(t1)] = k.alloc("ml_xb2", [128, 512], BF16)
        pts = [k.alloc(f"ml_pt{j}", [128, 512], BF16) for j in range(3)]; b_pts = [Buf(f"pt{j}") for j in range(3)]
        rl = k.alloc("ml_rl", [128, 512], F32); b_rl = Buf("rl")
        items = [(hd, s0, S, q0) for hd in range(8) for (s0, S) in seqs for q0 in range(s0, s0 + S, QT)]

        def kprep(hd):
            sl = hd % 2
            wuk = self.wtile(self.dr["mla_w_uk"], hd)
            wuv = self.wtile(self.dr["mla_w_uv"], hd)
            k.dma(k.POOL, wuqs[sl][:], self.dr["mla_w_uq"].ap()[hd], None, out_bufs=[b_wuq[sl]])

            def kn_chain(c0, ci):
                n = min(512, LT - c0)
                pv = self.psum(128, n, ring=self.SIDE_RING)
                k.mm(pv, [(V(wuk.ap[:, 0, kc, :], wuk.bufs), V(ckvn[:, kc, c0:c0 + n], b_ckvn)) for kc in range(2)])
                yield
                yield from self.rms_g([pv], 1, 128, 128, lambda c: self.vcol("mla_kn_nope"), [V(knh[sl][:, c0:c0 + n], b_knh[sl])],
                                      sqs[ci % 2])

            def v_chain(c0):
                nn = min(4, LC - c0)
                pv = self.psum(128, nn * 128, ring=self.SIDE_RING)
                for j in range(nn):
                    k.mm(V(pv.ap[:, j * 128:(j + 1) * 128], pv.bufs),
                         [(V(ckvn[:, kc, (c0 + j) * 128:(c0 + j + 1) * 128], b_ckvn), V(wuv.ap[:, 0, kc, :], wuv.bufs)) for kc in range(2)])
                yield
                k.copy(k.ACT, V(vh[sl][:, c0:c0 + nn, :], b_vh[sl]), V(pv.ap.rearrange("p (a b) -> p a b", b=128), pv.bufs))
                yield
            chains = [kn_chain(c0, ci) for ci, c0 in enumerate(range(0, LT, 512))] + [v_chain(c0) for c0 in range(0, LC, 4)]
            yield from self.rr(chains, 2)

        def qprep(n):
            hd, s0, S, q0 = items[n]
            sl, ws = n % 2, hd % 2
            wq, bq = wuqs[ws], b_wuq[ws]
            pvn = self.psum(128, QT, ring=self.SIDE_RING)
            k.mm(pvn, [(V(wq[:, kc, 0:128], bq), V(cqn[:, kc, q0:q0 + QT], b_cqn)) for kc in range(6)])
            yield
            yield from self.rms_g([pvn], 1, 128, 128, lambda c: self.vcol("mla_qn_nope"), [V(qn[sl][:, 0:QT], b_q[sl])], sqs[2],
                                  extra_scale=scale)
            pvr = self.psum(64, QT, ring=self.SIDE_RING)
            k.mm(pvr, [(V(wq[:, kc, 128:192], bq), V(cqn[:, kc, q0:q0 + QT], b_cqn)) for kc in range(6)])
            yield
            if sample:
                qf = V(qrf[sl][:, 0:QT], b_q[sl])
                yield from self.rms_g([pvr], 1, 64, 64, lambda c: self.vcol("mla_qn_rope", 0, 64), [qf], sqs[2], extra_scale=scale)
                yield from self.rope_g(qf, V(qr[sl][0:64, 0:QT], b_q[sl]), 64, q0, QT, (t1, b_t1))
            else:
                yield from self.rms_g([pvr], 1, 64, 64, lambda c: self.vcol("mla_qn_rope", 0, 64), [V(qr[sl][0:64, 0:QT], b_q[sl])], sqs[2],
                                      extra_scale=scale)

        def prep(n):
            hd = items[n][0]
            if n == 0 or items[n - 1][0] != hd:
                yield from kprep(hd)
            yield from qprep(n)

        self.drain(prep(0))
        for n, (hd, s0, S, q0) in enumerate(items):
            sl, ks = n % 2, hd % 2
            kcols = list(range(0, CTX, 128)) + [CTX + s0 + a for a in range(0, S, 128)]
            side, spi = None, 1
            if n + 1 < len(items):
                side = prep(n + 1)
                spi = 4 if items[n + 1][0] != hd else 1

            def sp(kc, kcols=kcols, sl=sl, ks=ks):
                c0 = kcols[kc]
                return [(V(knh[ks][:, c0:c0 + 128], b_knh[ks]), V(qn[sl][:, 0:QT], b_q[sl])),
                        (V(krall[:, c0:c0 + 128], b_kr), V(qr[sl][:, 0:QT], b_q[sl]))]
            side = self.attend(len(kcols), QT, sp, lambda kc, kcols=kcols, ks=ks: V(vh[ks][:, kcols[kc] // 128, :], b_vh[ks]),
                               pts, b_pts, V(oT[:, hd % 2, q0:q0 + QT], b_oT), (rl, b_rl), side=side, spi=spi)
            self.drain(side)
            last_of_head = (n + 1 == len(items)) or items[n + 1][0] != hd
            if last_of_head and hd % 2 == 1:
                self.out_proj(i, P, self.dr["mla_w_o"], oT, b_oT, 2, k0=hd - 1)

    def diff(self, i, P):
        k, nc = self.k, self.nc
        T, NT, seqs, sample, QT = P["T"], P["NT"], P["seqs"], P["sample"], P["QT"]
        CTX = 256 if sample else 0
        LT = T + CTX
        LC = LT // 128
        scale = 64.0 ** -0.5
        lam_init = 0.8 - 0.6 * math.exp(-0.3 * i)
        h = k.alloc("df_h", [128, 8, T], BF16); b_h = [Buf(f"dh{t}") for t in range(NT)]
        oT = k.alloc("df_oT", [128, 2, T], BF16); b_oT = Buf("doT")
        if sample:
            self.load_rope()
        m1 = k.mark()
        self.normmod_all(i, 0, P, h, b_h)
        k.barrier()
        k.release(m1)
        qP = k.alloc("df_q", [128, 2, 2, T], BF16); b_q = [Buf("dq0"), Buf("dq1")]
        k.memset(k.POOL, V(qP[64:128, 0], b_q[0]), 0.0)
        k.memset(k.POOL, V(qP[0:64, 1], b_q[1]), 0.0)
        kT = k.alloc("df_k", [128, 2, LT], BF16); b_k = [Buf("dk0"), Buf("dk1")]
        vv = k.alloc("df_v", [128, LC, 256], BF16); b_v = Buf("dv")
        W = 2
        scrs = [dict(xf=k.alloc(f"df_xf{j}", [128, 512], F32), b_xf=Buf(f"dxf{j}"),
                     sq=(k.alloc(f"df_sq{j}", [128, 1, 512], BF16), k.alloc(f"df_rs{j}", [128, 512], F32), Buf(f"dsq{j}")),
                     t1=(k.alloc(f"df_t1{j}", [128, 512], F32), Buf(f"dt1{j}"))) for j in range(W)]
        self._xb = getattr(self, "_xb", {})
        for S_ in scrs:
            self._xb[id(S_["t1"][0])] = k.alloc("df_xb", [128, 512], BF16)
        pts = [k.alloc(f"df_pt{j}", [128, 512], BF16) for j in range(3)]; b_pts = [Buf(f"dpt{j}") for j in range(3)]
        rl = k.alloc("df_rl", [128, 512], F32); b_rl = Buf("drl")
        o12 = [[k.alloc(f"df_o{a}{m}", [128, 512], F32) for m in range(2)] for a in range(2)]
        b_o = [Buf("do0"), Buf("do1")]
        vst = k.alloc("df_vst", [128, 256], F32) if not sample else None
        b_vst = Buf("dvst")
        wq = self.dr["diff_w_qkv"]
        dkT = self.dr["ctx_dkT"].ap().rearrange("(c p) t -> p c t", p=128)
        stdk = self.dr["st_dkT"].ap().rearrange("(c p) t -> p c t", p=128)
        for j in range(4):
            if sample:
                for mp in range(2):
                    k.dma(k.POOL, kT[:, mp, 0:256], dkT[:, mp * 4 + j, :], None, out_bufs=[b_k[mp]])
                k.dma(k.POOL, vv[:, 0:2, :], self.dr["ctx_dv"].ap().rearrange("(c p) f -> p c f", p=128)[:, :, j * 256:(j + 1) * 256],
                      None, out_bufs=[b_v])
            wcache = {}

            def getw(key, n0, nn=1):
                if key not in wcache:
                    wcache[key] = self.wtile(wq, n0, nn)
                return wcache[key]

            def qk_chain(which, mp, tt, ci):
                S_ = scrs[ci % W]
                dst, b_dst, off, gname = ((None, b_q[mp], 0, "diff_qn"), (kT, b_k[mp], CTX, "diff_kn"))[which]
                w = getw((which, mp), which * 8 + mp * 4 + j)
                sl = slice(tt * 512, (tt + 1) * 512)
                pv = self.psum(128, 512, ring=[0, 1, 2, 3])
                k.mm(pv, [(V(w.ap[:, 0, kc, :], w.bufs), V(h[:, kc, sl], b_h[tt])) for kc in range(8)])
                yield
                dsl = slice(off + tt * 512, off + (tt + 1) * 512)
                es = scale if which == 0 else 1.0
                gf = lambda c, g=gname: self.vcol(g)
                xv = V(S_["xf"][:], S_["b_xf"])
                qouts = None
                if which == 0:
                    qouts = [(0, 64, V(qP[0:64, 0, mp, dsl], b_q[mp])), (64, 128, V(qP[64:128, 1, mp, dsl], b_q[mp]))]
                if sample:
                    yield from self.rms_g([pv], 1, 128, 64, gf, [xv], S_["sq"], ones=self.ones64, extra_scale=es, ring=[4, 5, 6, 7])
                    yield from self.rope_g(xv, qouts if which == 0 else V(dst[:, mp, dsl], b_dst), 128, tt * 512, 512, S_["t1"],
                                           ring=[4, 5, 6, 7])
                elif which == 0:
                    yield from self.rms_g([pv], 1, 128, 64, gf, [xv], S_["sq"], ones=self.ones64, extra_scale=es, ring=[4, 5, 6, 7])
                    for (r0, r1, ov) in qouts:
                        k.copy(k.ACT, ov, V(S_["xf"][r0:r1, :], S_["b_xf"]))
                    yield
                else:
                    yield from self.rms_g([pv], 1, 128, 64, gf, [xv], S_["sq"], ones=self.ones64, ring=[4, 5, 6, 7])
                    k.copy(k.ACT, V(dst[:, mp, dsl], b_dst), xv)
                    self.store(stdk[:, mp * 4 + j, sl], S_["xf"][:], [S_["b_xf"]], "st")
                    yield

            def v_chain(a):
                tt = a // 4
                w = getw("v", 16 + 2 * j, 2)
                pv = self.psum(128, 256, ring=[0, 1, 2, 3])
                k.mm(V(pv.ap.rearrange("p (a b) -> p a b", b=128), pv.bufs),
                     [(V(h[:, kc, a * 128:(a + 1) * 128], b_h[tt]), V(w.ap[:, :, kc, :], w.bufs)) for kc in range(8)])
                yield
                k.copy(k.ACT, V(vv[:, CTX // 128 + a, :], b_v), pv)
                if not sample:
                    k.copy(k.DVE, V(vst[:], b_vst), pv)
                    self.store(self.dr["st_dv"].ap()[a * 128:(a + 1) * 128, j * 256:(j + 1) * 256], vst[:], [b_vst], "st")
                yield
            chains = []
            ci = 0
            for which in range(2):
                for mp in range(2):
                    for tt in range(NT):
                        chains.append((qk_chain, (which, mp, tt, ci)))
                        ci += 1
            for a in range(T // 128):
                chains.append((v_chain, (a,)))
            self.drain(self.rr((f(*args) for f, args in chains), W))
            items = [(hh, s0, S, q0) for hh in range(2) for (s0, S) in seqs for q0 in range(s0, s0 + S, QT)]

            def combine(n):
                hh, s0, S, q0 = items[n]
                o1, o2 = o12[n % 2]
                bo = b_o[n % 2]
                k.stt(k.DVE, V(o1[:, 0:QT], bo), V(o2[:, 0:QT], bo), V(self.lam[:, 1:2], self.b_lam), V(o1[:, 0:QT], bo), ALU.mult, ALU.add)
                yield
                yield from self.rms_g([V(o1[:, 0:QT], bo)], 1, 128, 128, lambda c: self.vcol("diff_hn"),
                                      [V(oT[:, hh, q0:q0 + QT], b_oT)], scrs[0]["sq"], extra_scale=(1.0 - lam_init), ring=[5, 6, 7])
            pending = None
            for n, (hh, s0, S, q0) in enumerate(items):
                pb = hh * 64
                kcols = list(range(0, CTX, 128)) + [CTX + s0 + a for a in range(0, S, 128)]
                for mp in range(2):
                    def sp(kc, kcols=kcols, mp=mp, hh=hh, q0=q0):
                        c0 = kcols[kc]
                        return [(V(kT[:, mp, c0:c0 + 128], b_k[mp]), V(qP[:, hh, mp, q0:q0 + QT], b_q[mp]))]
                    pending = self.attend(len(kcols), QT, sp,
                                          lambda kc, kcols=kcols, hh=hh: V(vv[:, kcols[kc] // 128, hh * 128:(hh + 1) * 128], b_v),
                                          pts, b_pts, V(o12[n % 2][mp][:, 0:QT], b_o[n % 2]), (rl, b_rl), side=pending, spi=1)
                self.drain(pending)
                pending = combine(n)
            self.drain(pending)
            self.out_proj(i, P, self.dr["diff_w_o"], oT, b_oT, 2, k0=2 * j)

    def gelu_tanh(self, out, x, scr, b):
        k = self.k
        k.act(scr, x, AF.Square)
        k.ts(k.DVE, scr, scr, 0.044715, ALU.mult, 1.0, ALU.add)
        k.tt(k.DVE, scr, scr, x, ALU.mult)
        k.act(scr, scr, AF.Sigmoid, scale=1.5957691216057308)
        k.tt(k.DVE, out, scr, x, ALU.mult)

    def gmlp(self, i, P):
        k, nc = self.k, self.nc
        T, NT = P["T"], P["NT"]
        ht = k.alloc("gm_h", [128, 8, 512], BF16); b_ht = Buf("ght")
        gated = k.alloc("gm_g", [128, 8, T], BF16); b_g = Buf("gmg")
        scr = self.nm_scratch()
        wv = k.alloc("gm_wv", [128, 8, 8, 128], BF16); b_wv = Buf("gmwv")
        k.dma(k.POOL, wv[:], self.dr["gmlp_w_in"].ap()[8:16].rearrange("n p k j -> p n k j"), None, out_bufs=[b_wv])
        wsT = k.alloc("gm_ws", [128, 8, 128], BF16); vnbc = k.alloc("gm_vn", [128, 1024], F32); bsbc = k.alloc("gm_bs", [128, 1024], F32)
        b_c = Buf("gmc")
        k.dma(k.POOL, wsT[:], self.dr["gmlp_wsT"].ap(), None, out_bufs=[b_c])
        k.dma(k.SP, vnbc[:], self.dr["gmlp_vn_bc"].ap(), None, out_bufs=[b_c])
        k.dma(k.SP, bsbc[:], self.dr["gmlp_bs_bc"].ap(), None, out_bufs=[b_c])
        nset = 2 if P["sample"] else 1
        sets = [dict(vt=k.alloc(f"gm_vt{j}", [128, 1024], F32), s1=k.alloc(f"gm_s1{j}", [128, 1024], F32),
                     vnb=k.alloc(f"gm_vnb{j}", [128, 1024], BF16), ssq=k.alloc(f"gm_ssq{j}", [128, 2], F32),
                     b_vt=Buf(f"gmvt{j}"), b_s1=Buf(f"gms1{j}"), b_vnb=Buf(f"gmvnb{j}"), b_ssq=Buf(f"gmssq{j}")) for j in range(nset)]
        mixed = k.alloc("gm_mixed", [128, 8, 512], F32); b_mx = Buf("gmmx")
        uf = k.alloc("gm_uf", [128, 512], F32); b_uf = Buf("gmuf")

        def gelu_g(items):
            for (o, x, sc) in items:
                k.act(sc, x, AF.Square)
            yield
            for (o, x, sc) in items:
                k.ts(k.DVE, sc, sc, 0.044715, ALU.mult, 1.0, ALU.add)
            yield
            for (o, x, sc) in items:
                k.tt(k.DVE, sc, sc, x, ALU.mult)
            yield
            for (o, x, sc) in items:
                k.act(sc, sc, AF.Sigmoid, scale=1.5957691216057308)
            yield
            for (o, x, sc) in items:
                k.tt(k.DVE, o, sc, x, ALU.mult)
            yield

        def vblock(tt, sb, S_):
            vt, s1, vnb, ssq = S_["vt"], S_["s1"], S_["vnb"], S_["ssq"]
            b_vt, b_s1, b_vnb, b_ssq = S_["b_vt"], S_["b_s1"], S_["b_vnb"], S_["b_ssq"]
            pvs = []
            for nb in range(2):
                pv = self.psum(128, 512, ring=[0, 1, 2, 3])
                k.mm(V(pv.ap.rearrange("p (a b) -> p a b", b=128), pv.bufs),
                     [(V(ht[:, kc, sb * 128:(sb + 1) * 128], b_ht), V(wv[:, nb * 4:(nb + 1) * 4, kc, :], b_wv)) for kc in range(8)])
                pvs.append(pv)
            yield
            yield from gelu_g([(V(vt[:, nb * 512:(nb + 1) * 512], b_vt), pvs[nb], V(s1[:, nb * 512:(nb + 1) * 512], b_s1)) for nb in range(2)])
            k.memset(k.DVE, V(ssq[:, 0:1], b_ssq), 0.0)
            k.op(k.ACT, [V(s1[:], b_s1), V(ssq[:, 0:1], b_ssq)], [V(vt[:], b_vt)],
                 lambda: nc.scalar.activation(out=s1[:], in_=vt[:], func=AF.Square, accum_out=ssq[:, 0:1]))
            yield
            k.act(V(ssq[:, 1:2], b_ssq), V(ssq[:, 0:1], b_ssq), AF.Ln, bias=self.cst(EPS), scale=1.0 / 1024)
            yield
            k.act(V(ssq[:, 1:2], b_ssq), V(ssq[:, 1:2], b_ssq), AF.Exp, scale=-0.5)
            yield
            k.stt(k.DVE, V(vnb[:], b_vnb), V(vt[:], b_vt), V(ssq[:, 1:2], b_ssq), V(vnbc[:], b_c), ALU.mult, ALU.mult)
            yield
            for nb in range(2):
                pv = self.psum(128, 512, ring=[4, 5])
                for gg in range(4):
                    g = nb * 4 + gg
                    k.mm(V(pv.ap[:, gg * 128:(gg + 1) * 128], pv.bufs), [(V(vnb[:, g * 128:(g + 1) * 128], b_vnb), V(wsT[:, g, :], b_c))])
                k.tt(k.DVE, V(mixed[:, nb * 4:(nb + 1) * 4, sb * 128:(sb + 1) * 128], b_mx),
                     V(pv.ap.rearrange("p (a b) -> p a b", b=128), pv.bufs),
                     V(bsbc[:, nb * 512:(nb + 1) * 512].rearrange("p (a b) -> p a b", b=128), b_c), ALU.add)
                yield

        def uchain(tt, g, ufv, scv):
            sl = slice(tt * 512, (tt + 1) * 512)
            w = self.wtile(self.dr["gmlp_w_in"], g)
            pv = self.psum(128, 512, ring=[6, 7])
            k.mm(pv, [(V(w.ap[:, 0, kc, :], w.bufs), V(ht[:, kc, :], b_ht)) for kc in range(8)])
            yield
            yield from gelu_g([(ufv, pv, scv)])
            k.tt(k.DVE, V(gated[:, g, sl], b_g), ufv, V(mixed[:, g, :], b_mx), ALU.mult)
            yield

        uscr = [(V(uf[:], b_uf), V(sets[0]["s1"][:, 0:512], sets[0]["b_s1"]))]
        if nset > 1:
            uscr.append((V(sets[1]["vt"][:, 0:512], sets[1]["b_vt"]), V(sets[1]["s1"][:, 0:512], sets[1]["b_s1"])))
        for tt in range(NT):
            self.normmod(i, 0, P, tt, V(ht[:], b_ht), scr)
            self.drain(self.rr([vblock(tt, sb, sets[sb % nset]) for sb in range(4)], nset))
            self.drain(self.rr([uchain(tt, g, *uscr[g % len(uscr)]) for g in range(8)], len(uscr)))
        self.out_proj(i, P, self.dr["gmlp_w_out"], gated, b_g, 8)


_CACHE = {}


def _get_prog(dbg=None):
    key = ("prog", dbg)
    if key not in _CACHE:
        _CACHE[key] = Prog(dbg)
    return _CACHE[key]


def _prep_shared(I):
    f = lambda a: np.ascontiguousarray(np.asarray(a, np.float32))
    sh = {}
    sh["ada_w"] = np.stack([_relayout_w(f(I["ada_w"][i])) for i in range(DEPTH)])
    sh["ffn_w_in"] = np.stack([_relayout_w(f(I["ffn_w_in"][i])) for i in range(DEPTH)])
    sh["ffn_w_out"] = np.stack([_relayout_w(f(I["ffn_w_out"][i])) for i in range(DEPTH)])
    wd = f(I["mla_w_down"][0])
    wd = np.concatenate([wd, np.zeros((D, 64), np.float32)], axis=1)
    sh["mla_w_down"] = _relayout_w(wd)
    uq = f(I["mla_w_uq"][0])
    sh["mla_w_uq"] = np.ascontiguousarray(uq.reshape(6, 128, 8, 192).transpose(2, 1, 0, 3))
    sh["mla_w_uk"] = _relayout_w(f(I["mla_w_uk"][0]))
    sh["mla_w_uv"] = _relayout_w(f(I["mla_w_uv"][0]))
    sh["mla_w_o"] = _relayout_w(f(I["mla_w_o"][0]))
    sh["diff_w_qkv"] = _relayout_w(f(I["diff_w_qkv"][0]))
    sh["diff_w_o"] = _relayout_w(f(I["diff_w_o"][0]))
    sh["sconv_w_in"] = _relayout_w(f(I["sconv_w_in"][0]))
    sh["sconv_w_out"] = _relayout_w(f(I["sconv_w_out"][0]))
    sh["gmlp_w_in"] = _relayout_w(f(I["gmlp_w_in"][0]))
    sh["gmlp_w_out"] = _relayout_w(f(I["gmlp_w_out"][0]))
    sh["gmlp_wsT"] = np.ascontiguousarray(f(I["gmlp_w_s"][0]).transpose(2, 0, 1))
    sh["gmlp_vn_bc"] = np.ascontiguousarray(np.broadcast_to(f(I["gmlp_v_norm"][0])[None, :], (128, 1024)))
    sh["gmlp_bs_bc"] = np.ascontiguousarray(np.broadcast_to(f(I["gmlp_b_s"][0]).reshape(1, 1024), (128, 1024)))
    vp = VecPack()
    for i in range(DEPTH):
        vp.add(f"n1g{i}", I["norm1_g"][i]); vp.add(f"n2g{i}", I["norm2_g"][i])
        for kk in range(3):
            vp.add(f"fcw{i}_{kk}", I["ffn_conv_w"][i][kk])
        vp.add(f"fcb{i}", I["ffn_conv_b"][i])
        vp.add(f"adab{i}", I["ada_b"][i])
    vp.add("mla_qnorm", I["mla_q_norm"][0]); vp.add("mla_kvnorm", I["mla_kv_norm"][0])
    vp.add("mla_qn_nope", I["mla_qn_nope"][0]); vp.add("mla_qn_rope", I["mla_qn_rope"][0])
    vp.add("mla_kn_nope", I["mla_kn_nope"][0]); vp.add("mla_kn_rope", I["mla_kn_rope"][0])
    vp.add("diff_qn", I["diff_qn"][0]); vp.add("diff_kn", I["diff_kn"][0]); vp.add("diff_hn", I["diff_head_norm"][0])
    for nm in ("lq1", "lk1", "lq2", "lk2"):
        vp.add("diff_" + nm, I["diff_" + nm][0])
    for kk in range(3):
        vp.add(f"scw{kk}", I["sconv_w"][0][kk])
    assert vp.idx == VIDX.idx
    sh["vecs"] = vp.pack()
    C, S = _rope_tables(2048)
    sh["ropeC"], sh["ropeS"] = C, S
    return sh


def _consts_host():
    ident = np.eye(128, dtype=np.float32)
    perm = np.zeros((128, 128), np.float32)
    for m in range(128):
        perm[m + 32 if m % 64 < 32 else m - 32, m] = 1.0
    o64 = np.zeros((128, 128), np.float32)
    o64[0:64, 0:64] = 1.0
    o64[64:, 64:] = 1.0
    return ident, perm, o64


def kernel(**I):
    dbg = I.pop("_dbg", None)
    ncores = I.pop("_ncores", NCORES)
    prog = _get_prog(dbg)
    f = lambda a: np.ascontiguousarray(np.asarray(a, np.float32))
    sh = _prep_shared(I)
    ident, perm, o64 = _consts_host()
    sh["c_ident"], sh["c_perm"], sh["c_ones64"] = ident, perm, o64
    in_maps = []
    for c in range(ncores):
        m = dict(sh)
        m["xT_p"] = np.ascontiguousarray(f(I["x_prompt"][2 * c:2 * c + 2]).reshape(512, D).T)
        m["xT_s"] = np.ascontiguousarray(f(I["x_sample"][c]).T)
        cond = np.stack([f(I["c_ctx"]), f(I["c"][c])], axis=-1)
        m["condT"] = np.ascontiguousarray(cond.reshape(8, 128, 2).transpose(1, 0, 2))
        m["ctx_ckvT"] = np.ascontiguousarray(f(I["cache_mla_ckv"][c, 0]).T)
        m["ctx_krT"] = np.ascontiguousarray(f(I["cache_mla_krope"][c, 0]).T)
        m["ctx_dkT"] = np.ascontiguousarray(f(I["cache_diff_k"][c, 0]).reshape(256, 1024).T)
        m["ctx_dv"] = np.ascontiguousarray(f(I["cache_diff_v"][c, 0]).reshape(256, 1024))
        in_maps.append(m)
    res = run_bass_kernel_spmd(prog.nc, in_maps, core_ids=list(range(ncores)))
    R = res.results
    NC_ = ncores
    yp = np.concatenate([R[c]["yT_p"].T.reshape(2, 256, D) for c in range(NC_)], axis=0)
    ys = np.stack([R[c]["yT_s"].T for c in range(NC_)], axis=0)
    ckv = np.concatenate([R[c]["st_ckvT"].T.reshape(2, 1, 256, 256) for c in range(NC_)], axis=0)
    kr = np.concatenate([R[c]["st_krT"].T.reshape(2, 1, 256, 64) for c in range(NC_)], axis=0)
    dk = np.concatenate([R[c]["st_dkT"].T.reshape(2, 1, 256, 2, 8, 64) for c in range(NC_)], axis=0)
    dv = np.concatenate([R[c]["st_dv"].reshape(2, 1, 256, 8, 128) for c in range(NC_)], axis=0)
    out = tuple(np.ascontiguousarray(a.astype(np.float32)) for a in (yp, ys, ckv, kr, dk, dv))
    return out
```
